# Optimizing a Trainium2 kernel written in Bass

```python
import jax, jax.numpy as jnp
from jax import lax
import numpy as np

D_MODEL = 2048
BATCH = 4
SEQ = 2048
DEPTH = 4

GRID_W = 64
N_HEADS = 64
HEAD_DIM = D_MODEL // N_HEADS
WIN_H = 8
WIN_W = 16
POOL_WINDOWS = (2, 4, 8, 16)
N_GROUPS = len(POOL_WINDOWS)
GROUP_CH = D_MODEL // N_GROUPS
D_FF = ((8 * D_MODEL + 3 * 256 - 1) // (3 * 256)) * 256
PLE_DIM = 256
N_MIXERS = 2
N_ATTN = (DEPTH + 1) // 2
N_POOL = DEPTH // 2
RMS_EPS = 1e-6

kernel_name = "hybrid_natten_poolformer_encoder"


def rms_norm(x, g):
    xf = x.astype(jnp.float32)
    y = xf * lax.rsqrt(jnp.mean(xf * xf, axis=-1, keepdims=True) + RMS_EPS)
    return (y * g.astype(jnp.float32)).astype(x.dtype)


def neighbourhood_attention(h, w_qkv, b_qkv, w_o, rpb):
    B, S, _ = h.shape
    rows = S // GRID_W
    kh = min(WIN_H, rows)
    qkv = (h @ w_qkv + b_qkv).reshape(B, S, 3, N_HEADS, HEAD_DIM)
    q = qkv[:, :, 0] * (HEAD_DIM ** -0.5)
    k = qkv[:, :, 1]
    v = qkv[:, :, 2]

    qc = jnp.arange(GRID_W)[:, None]
    kc = jnp.arange(GRID_W)[None, :]
    c_start = jnp.clip(qc - WIN_W // 2, 0, GRID_W - WIN_W)
    col_mask = (kc >= c_start) & (kc < c_start + WIN_W)
    dc_idx = jnp.clip(kc - qc, -(WIN_W - 1), WIN_W - 1) + (WIN_W - 1)
    mask = jnp.broadcast_to(col_mask[:, None, :], (GRID_W, kh, GRID_W)).reshape(GRID_W, kh * GRID_W)
    neg = jnp.finfo(jnp.float32).min

    def row_block(r):
        r_start = jnp.clip(r - kh // 2, 0, rows - kh)
        qb = lax.dynamic_slice_in_dim(q, r * GRID_W, GRID_W, axis=1)
        kb = lax.dynamic_slice_in_dim(k, r_start * GRID_W, kh * GRID_W, axis=1)
        vb = lax.dynamic_slice_in_dim(v, r_start * GRID_W, kh * GRID_W, axis=1)
        dr_idx = r_start + jnp.arange(kh) - r + (WIN_H - 1)
        bias = rpb[:, dr_idx[None, :, None], dc_idx[:, None, :]]
        bias = bias.reshape(N_HEADS, GRID_W, kh * GRID_W).astype(jnp.float32)
        s = jnp.einsum('bqhd,bkhd->bhqk', qb, kb).astype(jnp.float32) + bias[None]
        s = jnp.where(mask[None, None], s, neg)
        pr = jax.nn.softmax(s, axis=-1).astype(vb.dtype)
        return jnp.einsum('bhqk,bkhd->bqhd', pr, vb)

    o = lax.map(row_block, jnp.arange(rows))
    o = jnp.transpose(o, (1, 0, 2, 3, 4)).reshape(B, S, N_HEADS * HEAD_DIM)
    return o @ w_o


def multiscale_pool(h, w_pool, scale):
    B, S, D = h.shape
    hf = h.astype(jnp.float32)
    csum = jnp.concatenate([jnp.zeros((B, 1, D), jnp.float32), lax.cumsum(hf, axis=1)], axis=1)
    csum = csum.reshape(B, S + 1, N_GROUPS, GROUP_CH)
    t = jnp.arange(S)
    means = []
    for g, w in enumerate(POOL_WINDOWS):
        lo = jnp.clip(t - w // 2, 0, S)
        hi = jnp.clip(t + w - w // 2, 0, S)
        cnt = (hi - lo).astype(jnp.float32)
        cg = csum[:, :, g]
        seg = jnp.take(cg, hi, axis=1) - jnp.take(cg, lo, axis=1)
        means.append(seg / cnt[None, :, None])
    pooled = jnp.stack(means, axis=2)
    mixed = (pooled - hf.reshape(B, S, N_GROUPS, GROUP_CH)).astype(h.dtype)
    out = jnp.einsum('bsgc,gce->bsge', mixed, w_pool).reshape(B, S, D)
    return out * scale


def swiglu(h, w_gate, w_up, w_down):
    return (jax.nn.silu(h @ w_gate) * (h @ w_up)) @ w_down


def setup_inputs(seed: int = 0) -> dict:
    key = jax.random.key(seed)
    ks = jax.random.split(key, 24)
    f32 = jnp.float32
    nrm = lambda k, shape, s: jax.random.normal(k, shape, f32) * s
    gain = lambda k, shape: 1.0 + 0.01 * jax.random.normal(k, shape, f32)
    return {
        "x": nrm(ks[0], (BATCH, SEQ, D_MODEL), 1.0),
        "p": nrm(ks[1], (DEPTH, BATCH, SEQ, PLE_DIM), 1.0),
        "attn_norm_g": gain(ks[2], (N_ATTN, D_MODEL)),
        "w_qkv": nrm(ks[3], (N_ATTN, D_MODEL, 3 * D_MODEL), D_MODEL ** -0.5),
        "b_qkv": nrm(ks[4], (N_ATTN, 3 * D_MODEL), 0.01),
        "w_o": nrm(ks[5], (N_ATTN, D_MODEL, D_MODEL), D_MODEL ** -0.5),
        "rpb": nrm(ks[6], (N_ATTN, N_HEADS, 2 * WIN_H - 1, 2 * WIN_W - 1), 0.1),
        "pool_norm_g": gain(ks[7], (N_POOL, D_MODEL)),
        "w_pool": nrm(ks[8], (N_POOL, N_GROUPS, GROUP_CH, GROUP_CH), GROUP_CH ** -0.5),
        "pool_scale": 1.0 + 0.1 * jax.random.normal(ks[9], (N_POOL, D_MODEL), f32),
        "ffn_norm_g": gain(ks[10], (DEPTH, D_MODEL)),
        "w_gate": nrm(ks[11], (DEPTH, D_MODEL, D_FF), D_MODEL ** -0.5),
        "w_up": nrm(ks[12], (DEPTH, D_MODEL, D_FF), D_MODEL ** -0.5),
        "w_down": nrm(ks[13], (DEPTH, D_FF, D_MODEL), D_FF ** -0.5),
        "ple_norm_g": gain(ks[14], (DEPTH, D_MODEL)),
        "w_ple_gate": nrm(ks[15], (DEPTH, D_MODEL, D_MODEL), D_MODEL ** -0.5),
        "b_ple_gate": nrm(ks[16], (DEPTH, D_MODEL), 0.01),
        "w_ple_proj": nrm(ks[17], (DEPTH, PLE_DIM, D_MODEL), PLE_DIM ** -0.5),
        "final_norm_g": gain(ks[18], (D_MODEL,)),
    }


def reference(x, p, attn_norm_g, w_qkv, b_qkv, w_o, rpb, pool_norm_g, w_pool, pool_scale,
              ffn_norm_g, w_gate, w_up, w_down, ple_norm_g, w_ple_gate, b_ple_gate,
              w_ple_proj, final_norm_g):
    for i in range(DEPTH):
        j = i // N_MIXERS
        if i % N_MIXERS == 0:
            x = x + neighbourhood_attention(rms_norm(x, attn_norm_g[j]), w_qkv[j], b_qkv[j], w_o[j], rpb[j])
        else:
            x = x + multiscale_pool(rms_norm(x, pool_norm_g[j]), w_pool[j], pool_scale[j])
        x = x + swiglu(rms_norm(x, ffn_norm_g[i]), w_gate[i], w_up[i], w_down[i])
        gate = jax.nn.sigmoid(rms_norm(x, ple_norm_g[i]) @ w_ple_gate[i] + b_ple_gate[i])
        x = x + (p[i] @ w_ple_proj[i]) * gate
    return rms_norm(x, final_norm_g)
```

```python
import numpy as np
from contextlib import ExitStack
import concourse.bass as bass
import concourse.mybir as mybir
from concourse.bass_utils import run_bass_kernel_spmd

F32 = mybir.dt.float32
BF16 = mybir.dt.bfloat16
AF = mybir.ActivationFunctionType
ALU = mybir.AluOpType

D = 2048
NCH = 16
DFF = 5632
NF = 44
DEPTH = 4
SEQ = 2048
GW = 64
EPS = 1e-6
NEG = -30000.0
POOL_W = (2, 4, 8, 16)

T_KV = {0: 1664, 2: 1344}
T_OUT = {0: 1352, 1: 1344, 2: 1032, 3: 1024}
T_IN = {0: 1352, 1: 1352, 2: 1344, 3: 1032}
TX = 1352
TX0 = 1664
ARENA_BYTES = 209920

VC = {}
_c = 0
for _i in range(DEPTH):
    for _nm in ("mix_g", "ffn_g", "ple_g", "ple_b"):
        VC[(_nm, _i)] = _c
        _c += 16
for _j in range(2):
    for _nm in ("bq", "bk", "pscale"):
        VC[(_nm, _j)] = _c
        _c += 16
VC["final_g"] = _c
_c += 16
VC["cA"] = _c
VC["cB"] = _c + 1
_c += 2
VC["corr"] = _c
_c += 32
VC["eps"] = _c
_c += 1
VC["hmask"] = _c
_c += 4
NVEC = _c


def blocks(T, maxn=512):
    nb = -(-T // maxn)
    base = -(-T // nb)
    base = -(-base // 8) * 8
    out = []
    s = 0
    while s < T:
        n = min(base, T - s)
        out.append((s, n))
        s += n
    return out


class Slot:
    def __init__(self, ap=None, **kw):
        self.ap = ap
        self.w = []
        self.r = []
        self.__dict__.update(kw)

    def wwaits(self):
        return list(self.r) + list(self.w)

    def set_w(self, *toks):
        self.w = [t for t in toks if t is not None]
        self.r = []

    def add_w(self, tok):
        self.w.append(tok)

    def rwaits(self):
        return list(self.w)

    def add_r(self, tok):
        if tok is not None:
            self.r.append(tok)


class Ring:
    def __init__(self, slots):
        self.slots = slots
        self.i = 0

    def next(self):
        s = self.slots[self.i % len(self.slots)]
        self.i += 1
        return s


class Eng:
    def __init__(self, prog, name, semi):
        self.prog = prog
        self.name = name
        self.semi = semi
        self.ops = []
        self.waited = {}

    def do(self, fn, waits=(), sig=False, inc=None, nobar=False):
        ws = []
        allw = list(waits)
        if not nobar:
            allw += self.prog.barrier
        for t in allw:
            if t is None:
                continue
            s, v = t
            if self.waited.get(s, 0) >= v:
                continue
            self.waited[s] = v
            ws.append((s, v))
        op = {"fn": fn, "ws": ws, "inc": None}
        self.ops.append(op)
        if inc is not None:
            return self._inc(op, inc)
        if sig:
            return self._inc(op, (self.semi, 1))
        return None

    def _inc(self, op, inc):
        assert op["inc"] is None
        op["inc"] = inc
        self.prog.semcount[inc[0]] += inc[1]
        op["tok"] = (inc[0], self.prog.semcount[inc[0]])
        return op["tok"]

    def last_token(self):
        if not self.ops:
            return None
        op = self.ops[-1]
        if op["inc"] is None:
            return self._inc(op, (self.semi, 1))
        return op["tok"]


class Prog:
    def __init__(self, nc, stack):
        self.nc = nc
        self.stack = stack
        self.sems = []
        self.semnames = {}
        self.semcount = []
        self.barrier = []
        self.pe = Eng(self, "pe", self.new_sem("pe"))
        self.act = Eng(self, "act", self.new_sem("act"))
        self.dve = Eng(self, "dve", self.new_sem("dve"))
        self.sp = Eng(self, "sp", None)
        self.pool = Eng(self, "pool", None)

    def new_sem(self, name):
        if name in self.semnames:
            return self.semnames[name]
        self.semnames[name] = len(self.sems)
        s = self.stack.enter_context(self.nc.semaphore(name))
        self.sems.append(s)
        self.semcount.append(0)
        return len(self.sems) - 1

    def phase_barrier(self):
        toks = [e.last_token() for e in (self.pe, self.act, self.dve)]
        self.barrier = [t for t in toks if t is not None]

    def replay(self):
        sems = self.sems

        def run(E, e):
            for op in E.ops:
                for (s, v) in op["ws"]:
                    e.wait_ge(sems[s], v)
                ins = op["fn"](e)
                if op["inc"] is not None:
                    ins.then_inc(sems[op["inc"][0]], op["inc"][1])

        with self.nc.Block() as block:
            @block.tensor
            def _(e):
                run(self.pe, e)

            @block.scalar
            def _(e):
                run(self.act, e)

            @block.vector
            def _(e):
                run(self.dve, e)

            @block.sync
            def _(e):
                run(self.sp, e)

            @block.gpsimd
            def _(e):
                run(self.pool, e)


class Arena:
    def __init__(self, ap, nbytes):
        self.ap = ap
        self.nbytes = nbytes

    def view(self, off, shape, dtype):
        n = int(np.prod(shape))
        assert off % 4 == 0
        if dtype == BF16:
            assert off + 2 * n <= self.nbytes, (off, shape)
            v = self.ap[:, off // 2: off // 2 + n]
        else:
            assert off + 4 * n <= self.nbytes, (off, shape)
            v = self.ap[:, off // 2: off // 2 + 2 * n].bitcast(F32)
        if len(shape) == 2:
            v = v.rearrange("p (a b) -> p a b", b=shape[1])
        elif len(shape) == 3:
            v = v.rearrange("p (a b c) -> p a b c", b=shape[1], c=shape[2])
        return v


class Alloc:
    def __init__(self, arena, base):
        self.arena = arena
        self.off = base

    def take(self, shape, dtype):
        n = int(np.prod(shape)) * (2 if dtype == BF16 else 4)
        n = -(-n // 64) * 64
        v = self.arena.view(self.off, shape, dtype)
        self.off += n
        assert self.off <= self.arena.nbytes - 4 * 4096, ("arena overflow", self.off)
        return v


def build_program(stop_after=None):
    nc = bass.Bass("TRN2", target_bir_lowering=False)
    stack = ExitStack()
    dt = nc.dram_tensor
    xT = dt("xT", [NCH, 128, TX0], F32, kind="ExternalInput").ap()
    pT = dt("pT", [DEPTH, 2, 128, TX], F32, kind="ExternalInput").ap()
    vecs_d = dt("vecs", [128, NVEC], F32, kind="ExternalInput").ap()
    bv_d = dt("bv", [2, 128, D], F32, kind="ExternalInput").ap()
    ident_d = dt("ident", [128, 128], F32, kind="ExternalInput").ap()
    wqkv_d = dt("wqkv", [2, 16, 3, 128, 2048], F32, kind="ExternalInput").ap()
    wo_d = dt("wo", [2, 8, 128, 2, D], F32, kind="ExternalInput").ap()
    tab0_d = dt("tab0", [2, 16, 128, 4 * 4 * 128], F32, kind="ExternalInput").ap()
    tab1_d = dt("tab1", [2, 16, 128, 4 * 4 * 128], F32, kind="ExternalInput").ap()
    tab2_d = dt("tab2", [2, 16, 128, 4 * 5 * 128], F32, kind="ExternalInput").ap()
    wpool_d = dt("wpool", [2, 128, 16 * 512], F32, kind="ExternalInput").ap()
    wgate_d = dt("wgate", [DEPTH, NF, 128, 2048], F32, kind="ExternalInput").ap()
    wup_d = dt("wup", [DEPTH, NF, 128, 2048], F32, kind="ExternalInput").ap()
    wdown_d = dt("wdown", [DEPTH, 11, 16, 128, 4 * 128], F32, kind="ExternalInput").ap()
    wpg_d = dt("wpg", [DEPTH, 16, 128, 2048], F32, kind="ExternalInput").ap()
    wpp_d = dt("wpp", [DEPTH, 128, 2 * D], F32, kind="ExternalInput").ap()
    T_DUMP = 1024 if stop_after is None else TX
    outT = dt("outT", [NCH, 128, T_DUMP], F32, kind="ExternalOutput").ap()

    P = Prog(nc, stack)
    pe, act, dve, sp, pool = P.pe, P.act, P.dve, P.sp, P.pool
    arena_t = stack.enter_context(nc.sbuf_tensor("arena", [128, ARENA_BYTES // 2], BF16))
    arena = Arena(arena_t, ARENA_BYTES)
    vecs = stack.enter_context(nc.sbuf_tensor("vecs_sb", [128, NVEC], F32))
    ones = stack.enter_context(nc.sbuf_tensor("ones_sb", [128, 128], BF16))
    ident = stack.enter_context(nc.sbuf_tensor("ident_sb", [128, 128], BF16))
    bqs = stack.enter_context(nc.sbuf_tensor("bqs_sb", [128, 16], F32))
    sbig = stack.enter_context(nc.psum_tensor("sbig", [128, 2048], F32))
    banks = [sbig[:, i * 512:(i + 1) * 512] for i in range(4)]
    banks += [stack.enter_context(nc.psum_tensor(f"bank{i}", [128, 512], F32))[:, :] for i in range(4, 8)]
    sview = sbig[:, :].rearrange("p (a s b) -> p a s b", s=4, b=128)

    XSZ = NCH * TX * 4
    X = arena.view(0, (NCH, TX), F32)
    NWT = 4
    WT_BASE = ARENA_BYTES - NWT * 4096
    wt_slots = [Slot(arena.view(WT_BASE + i * 4096, (2048,), BF16), sem=P.new_sem(f"wt{i}")) for i in range(NWT)]
    wt_ring = Ring(wt_slots)

    def vcol(key, k=0, n=1):
        c = VC[key] + k
        return vecs[:, c:c + n]

    def dma_in(E, slot, out_ap, in_ap, nobar=False, extra_waits=()):
        tok = E.do(lambda e, o=out_ap, i=in_ap: e.dma_start(out=o, in_=i),
                   waits=slot.wwaits() + list(extra_waits), inc=(slot.sem, 16), nobar=nobar)
        slot.set_w(tok)
        return tok

    def load_wtile(src_ap):
        s = wt_ring.next()
        dma_in(pool, s, s.ap, src_ap, nobar=True)
        return s

    def mm(out, lhsT, rhs, start, stop, waits=(), sig=False, tp=None):
        if tp is None:
            fn = lambda e: e.matmul(out, lhsT=lhsT, rhs=rhs, start=start, stop=stop)
        else:
            fn = lambda e: e.matmul(out, lhsT=lhsT, rhs=rhs, start=start, stop=stop, tile_position=tp)
        return pe.do(fn, waits=waits, sig=sig)

    def bank_ring(idxs):
        return Ring([Slot(banks[i]) for i in idxs])

    vec_slot = Slot(sem=P.new_sem("vecs"))
    dma_in(sp, vec_slot, vecs[:, :], vecs_d[:, :])
    x_slot = Slot(sem=P.new_sem("xload"))
    xtoks = []
    for k in range(NCH):
        xtoks.append(sp.do(lambda e, k=k: e.dma_start(out=X[:, k, :], in_=xT[k, :, 0:TX]),
                           inc=(x_slot.sem, 16)))
    x_ready = [xtoks[-1]]
    ones_tok = dve.do(lambda e: e.memset(ones[:, :], 1.0), sig=True)
    id_slot = Slot(sem=P.new_sem("ident"))
    dma_in(pool, id_slot, ident[:, :], ident_d[:, :])
    const_ready = [ones_tok] + vec_slot.rwaits() + id_slot.rwaits()

    def norm_phase(T, gkey, H, al, src=None, tok_off=0, keep_rstd=False, out_f32_X=False, extra_waits=()):
        if src is None:
            src = lambda k, s, n: X[:, k, s:s + n]
        SQ = Ring([Slot(al.take((512,), BF16)) for _ in range(3)])
        TMP = Slot(al.take((512,), F32))
        RSTD = al.take((T,), F32)
        ring = bank_ring([6, 7])
        ew = list(extra_waits) + const_ready + x_ready
        for (s, n) in blocks(T):
            ps = ring.next()
            for k in range(NCH):
                sq = SQ.next()
                t = act.do(lambda e, o=sq.ap[:, 0:n], i=src(k, s, n): e.activation(out=o, in_=i, func=AF.Square),
                           waits=sq.wwaits() + ew, sig=True)
                sq.set_w(t)
                t = mm(ps.ap[:, 0:n], ones[:, :], sq.ap[:, 0:n], k == 0, k == NCH - 1,
                       waits=sq.rwaits() + (ps.wwaits() if k == 0 else []) + ew, sig=True)
                sq.add_r(t)
            ps.set_w(t)
            t = act.do(lambda e, o=TMP.ap[:, 0:n], i=ps.ap[:, 0:n]: e.activation(
                out=o, in_=i, func=AF.Sqrt, bias=vcol("eps"), scale=1.0 / D),
                waits=ps.rwaits() + TMP.wwaits(), sig=True)
            ps.add_r(t)
            TMP.set_w(t)
            t = dve.do(lambda e, o=RSTD[:, s:s + n], i=TMP.ap[:, 0:n]: e.reciprocal(out=o, in_=i),
                       waits=TMP.rwaits(), sig=True)
            TMP.add_r(t)
            if H is not None:
                for k in range(NCH):
                    o = X[:, k, s:s + n] if out_f32_X else H[:, k, tok_off + s: tok_off + s + n]
                    t2 = dve.do(lambda e, o=o, i=src(k, s, n), g=vcol(gkey, k), r=RSTD[:, s:s + n]:
                                e.scalar_tensor_tensor(out=o, in0=i, scalar=g, in1=r, op0=ALU.mult, op1=ALU.mult),
                                waits=[t] + ew)
        return RSTD, t

    def attention_layer(i):
        j = i // 2
        Tkv, Tq = T_KV[i], T_OUT[i]
        H = arena.view(XSZ, (NCH, Tkv), BF16)
        base = XSZ + NCH * Tkv * 2
        P.phase_barrier()
        al = Alloc(arena, base)
        if i == 0:
            norm_phase(TX, ("mix_g", i), H, al)
            nh = Tkv - TX
            XH = al.take((NCH, nh), F32)
            xh_slot = Slot(sem=P.new_sem("xh"))
            for k in range(NCH):
                tk = sp.do(lambda e, k=k: e.dma_start(out=XH[:, k, :], in_=xT[k, :, TX:Tkv]),
                           inc=(xh_slot.sem, 16))
            al2 = Alloc(arena, al.off)
            norm_phase(nh, ("mix_g", i), H, al2, src=lambda k, s, n: XH[:, k, s:s + n], tok_off=TX,
                       extra_waits=[tk])
        else:
            norm_phase(Tkv, ("mix_g", i), H, al)
        if stop_after == (i, "n"):
            return
        P.phase_barrier()
        al = Alloc(arena, base)
        nkc = Tkv // 128 if Tkv % 128 == 0 else Tkv // 128 + 1
        scale = 32 ** -0.5
        QKV = []
        for b in range(2):
            QKV.append(dict(Q=Slot(al.take((Tq,), BF16)), K=Slot(al.take((Tkv,), BF16)),
                            V=Slot(al.take((nkc, 128), BF16))))
        TB = [Slot(al.take((4 * 512,), BF16), sem=P.new_sem("tb0")),
              Slot(al.take((4 * 512,), BF16), sem=P.new_sem("tb1")),
              Slot(al.take((5 * 512,), BF16), sem=P.new_sem("tb2"))]
        tbq = dve.do(lambda e: e.tensor_scalar(out=bqs[:, :], in0=vcol(("bq", j), 0, 16), scalar1=scale, scalar2=None,
                                               op0=ALU.mult), waits=const_ready, sig=True)
        tabs_d = [tab0_d, tab1_d, tab2_d]
        OT = [Slot(al.take((2, Tq), BF16))]
        WO = Slot(al.take((2, D), BF16), sem=P.new_sem("wo"))
        PR = Ring([Slot(al.take((4, 128), BF16)) for _ in range(3)])
        RC = Ring([Slot(al.take((128,), F32)) for _ in range(2)])
        BVB = Ring([Slot(al.take((128,), F32), sem=P.new_sem(f"bvb{q}")) for q in range(2)])
        s_ring = bank_ring([0, 1])
        o_ring = Ring([Slot((banks[2], banks[3])), Slot((banks[4], banks[5]))])
        g_ring = bank_ring([6, 7])
        QB = Ring([Slot(al.take((4, 128), BF16)) for _ in range(2)])
        for qbs in QB.slots:
            qbs.set_w(dve.do(lambda e, o=qbs.ap: e.memset(o, 0.0), sig=True))
        qblocks = blocks(Tq)
        kblocks = blocks(Tkv)
        wtiles = {}
        bvbs = {}

        def dma_w(g):
            wtiles[g] = [load_wtile(wqkv_d[j, g, s]) for s in range(3)]
            sl = BVB.next()
            dma_in(sp, sl, sl.ap, bv_d[j, :, g * 128:(g + 1) * 128])
            bvbs[g] = sl

        def dma_tab(g):
            for t in range(3):
                sl = TB[t]
                dma_in(pool, sl, sl.ap, tabs_d[t][j, g])

        def qkv(g):
            buf = QKV[g % 2]
            wq, wk, wv = wtiles[g]
            for (dst, w, bkey, blks) in ((buf["Q"], wq, None, qblocks), (buf["K"], wk, "bk", kblocks)):
                first = True
                toks = []
                for (s, n) in blks:
                    ps = g_ring.next()
                    for k in range(NCH):
                        t = mm(ps.ap[:, 0:n], w.ap[:, k * 128:(k + 1) * 128], H[:, k, s:s + n], k == 0, k == NCH - 1,
                               waits=(ps.wwaits() + w.rwaits()) if k == 0 else (), sig=(k == NCH - 1))
                    ps.set_w(t)
                    bb = bqs[:, g:g + 1] if bkey is None else vcol((bkey, j), g)
                    sc = scale if bkey is None else 1.0
                    t2 = act.do(lambda e, o=dst.ap[:, s:s + n], i=ps.ap[:, 0:n], b=bb, sc=sc:
                                e.activation(out=o, in_=i, func=AF.Identity, bias=b, scale=sc),
                                waits=ps.rwaits() + (dst.wwaits() if first else []) + [tbq], sig=True)
                    first = False
                    ps.add_r(t2)
                    toks.append(t2)
                w.add_r(t)
                dst.set_w(toks[-1])
            dstv = buf["V"]
            bvb = bvbs[g]
            first = True
            for c0 in range(0, nkc, 4):
                ps = g_ring.next()
                ncs = min(4, nkc - c0)
                for cc in range(ncs):
                    c = c0 + cc
                    nk = min(128, Tkv - c * 128)
                    for k in range(NCH):
                        t = mm(ps.ap[0:nk, cc * 128:(cc + 1) * 128], H[:, k, c * 128:c * 128 + nk],
                               wv.ap[:, k * 128:(k + 1) * 128], k == 0, k == NCH - 1,
                               waits=(ps.wwaits() + wv.rwaits()) if (k == 0 and cc == 0) else (),
                               sig=(k == NCH - 1 and cc == ncs - 1))
                ps.set_w(t)
                for cc in range(ncs):
                    c = c0 + cc
                    nk = min(128, Tkv - c * 128)
                    t2 = dve.do(lambda e, o=dstv.ap[0:nk, c, :], a=ps.ap[0:nk, cc * 128:(cc + 1) * 128], b=bvb.ap[0:nk, :]:
                                e.tensor_tensor(out=o, in0=a, in1=b, op=ALU.add),
                                waits=ps.rwaits() + bvb.rwaits() + (dstv.wwaits() if first else []), sig=(cc == ncs - 1))
                    first = False
                ps.add_r(t2)
            wv.add_r(t)
            bvb.add_r(t2)
            dstv.set_w(t2)

        def attend(g):
            buf = QKV[g % 2]
            Qs, Ks, Vs = buf["Q"], buf["K"], buf["V"]
            ot = OT[0]
            gi = g % 2
            npair = -(-Tq // 128)
            last_pv = None
            q_last_tok = [None]
            for pj in range(npair):
                q0 = pj * 128
                nq = min(128, Tq - q0)
                if pj < 2:
                    tb, cs, ncz = TB[pj], 0, 4
                else:
                    tb, cs, ncz = TB[2], pj - 2, 5
                ob = o_ring.next()
                chunks = []
                for c in range(ncz):
                    kc = cs + c
                    nk = min(128, Tkv - kc * 128)
                    assert nk in (64, 128)
                    chunks.append((c, kc, nk))
                qb = QB.next()
                qb_last = [None]
                for hh in range(4):
                    tq = dve.do(lambda e, o=qb.ap[:, hh, 0:nq], i=Qs.ap[:, q0:q0 + nq], m=vcol("hmask", hh):
                                e.tensor_scalar(out=o, in0=i, scalar1=m, scalar2=None, op0=ALU.mult),
                                waits=(qb.wwaits() + Qs.rwaits()) if hh == 0 else (), sig=(hh == 3))
                qb.set_w(tq)
                q_last_tok[0] = tq

                def issue_s(c, kc, nk):
                    sb = s_ring.next()
                    sbv = sb.ap.rearrange("p (a b) -> p a b", b=128)[0:nk, :, 0:nq]
                    t = mm(sb.ap[0:nk, :], Ks.ap[:, kc * 128:kc * 128 + nk], qb.ap.rearrange("p a b -> p (a b)"), True, False,
                           waits=sb.wwaits() + qb.rwaits() + Ks.rwaits())
                    t = mm(sb.ap[0:nk, :], ident[0:nk, 0:nk], tb.ap[0:nk, c * 512:(c + 1) * 512], False, True,
                           waits=tb.rwaits(), sig=True)
                    qb_last[0] = t
                    sb.set_w(t)
                    pr = PR.next()
                    t2 = act.do(lambda e, o=pr.ap[0:nk, :, 0:nq], i=sbv: e.activation(out=o, in_=i, func=AF.Exp),
                                waits=sb.rwaits() + pr.wwaits(), sig=True)
                    sb.add_r(t2)
                    pr.set_w(t2)
                    return pr

                def issue_pv(c, kc, nk, pr):
                    nonlocal last_pv
                    for hh in range(4):
                        w = (pr.rwaits() + Vs.rwaits() + (ob.wwaits() if c == 0 else [])) if hh == 0 else ()
                        mm(ob.ap[0][32 * hh:32 * hh + 32, 0:nq], Vs.ap[0:nk, kc, 32 * hh:32 * hh + 32], pr.ap[0:nk, hh, 0:nq],
                           c == 0, c == ncz - 1, waits=w, tp=(0, 32 * hh))
                        t = mm(ob.ap[1][32 * hh:32 * hh + 32, 0:nq], ones[0:nk, 0:32], pr.ap[0:nk, hh, 0:nq],
                               c == 0, c == ncz - 1, sig=(hh == 3), tp=(0, 32 * hh))
                    pr.add_r(t)
                    last_pv = t
                    return t

                prs = {}
                prs[0] = issue_s(*chunks[0])
                for c in range(ncz):
                    if c + 1 < ncz:
                        prs[c + 1] = issue_s(*chunks[c + 1])
                    t = issue_pv(*chunks[c], prs[c])
                ob.set_w(t)
                qb.add_r(qb_last[0])
                rc = RC.next()
                t1 = dve.do(lambda e, o=rc.ap[:, 0:nq], i=ob.ap[1][:, 0:nq]: e.reciprocal(out=o, in_=i),
                            waits=ob.rwaits() + rc.wwaits(), sig=True)
                t2 = dve.do(lambda e, o=ot.ap[:, gi, q0:q0 + nq], a=ob.ap[0][:, 0:nq], b=rc.ap[:, 0:nq]:
                            e.tensor_tensor(out=o, in0=a, in1=b, op=ALU.mult),
                            waits=[t1] + (ot.wwaits() if (pj == 0 and gi == 0) else []), sig=True)
                rc.set_w(t2)
                ob.add_r(t2)
            Qs.add_r(q_last_tok[0])
            Ks.add_r(last_pv)
            Vs.add_r(last_pv)
            if gi == 0:
                ot.set_w(t2)
            else:
                ot.add_w(t2)

        def wo_dma(s):
            flat = WO.ap.rearrange("p a b -> p (a b)")
            dma_in(pool, WO, flat, wo_d[j, s].rearrange("p a b -> p (a b)"))

        def wo_set(s):
            ot = OT[0]
            for dc in range(NCH):
                for (s0, n) in qblocks:
                    ps = g_ring.next()
                    for gi in range(2):
                        t = mm(ps.ap[:, 0:n], WO.ap[:, gi, dc * 128:(dc + 1) * 128], ot.ap[:, gi, s0:s0 + n], gi == 0, gi == 1,
                               waits=(ps.wwaits() + WO.rwaits() + ot.rwaits()) if gi == 0 else (), sig=(gi == 1))
                    ps.set_w(t)
                    t2 = dve.do(lambda e, o=X[:, dc, s0:s0 + n], a=ps.ap[:, 0:n]:
                                e.tensor_tensor(out=o, in0=a, in1=o, op=ALU.add),
                                waits=ps.rwaits() + x_ready, sig=True)
                    ps.add_r(t2)
            WO.add_r(t)
            ot.add_r(t)

        def attend_body(g):
            attend(g)
            tok = pe.last_token()
            for t in range(3):
                TB[t].add_r(tok)

        dma_w(0)
        dma_tab(0)
        qkv(0)
        if stop_after == (i, "q"):
            return
        if stop_after == (i, "a"):
            attend_body(0)
            return
        if 1 < 16:
            dma_w(1)
        for g in range(16):
            if g + 1 < 16:
                qkv(g + 1)
            if g + 2 < 16:
                dma_w(g + 2)
            if g % 2 == 1:
                wo_dma(g // 2)
            attend_body(g)
            if g + 1 < 16:
                dma_tab(g + 1)
            if g % 2 == 1:
                wo_set(g // 2)

    def pool_layer(i):
        j = i // 2
        Tin, Tout = T_IN[i], T_OUT[i]
        H = arena.view(XSZ, (NCH, Tin), BF16)
        base = XSZ + NCH * Tin * 2
        P.phase_barrier()
        al = Alloc(arena, base)
        WP = Slot(al.take((16, 512), BF16), sem=P.new_sem("wp"))
        dma_in(pool, WP, WP.ap.rearrange("p a b -> p (a b)"), wpool_d[j])
        RSTD, rtok = norm_phase(Tin, None, None, al)
        PADL = 8
        HF = [Slot(al.take((PADL + Tin,), F32)) for _ in range(2)]
        PA = al.take((PADL + Tin,), F32)
        PB = al.take((PADL + Tin,), F32)
        for hf in HF:
            t = dve.do(lambda e, o=hf.ap[:, 0:PADL]: e.memset(o, 0.0), sig=True)
            hf.set_w(t)
        ring = bank_ring([0, 1, 2, 3])
        oblocks = blocks(Tout)
        prev_tl = rtok
        for k in range(NCH):
            grp = k // 4
            w = POOL_W[grp]
            hf = HF[k % 2]
            h = hf.ap
            tl = dve.do(lambda e, o=h[:, PADL:PADL + Tin], x=X[:, k, 0:Tin], g=vcol(("mix_g", i), k), r=RSTD[:, 0:Tin]:
                        e.scalar_tensor_tensor(out=o, in0=x, scalar=g, in1=r, op0=ALU.mult, op1=ALU.mult),
                        waits=hf.wwaits() + const_ready + [prev_tl], sig=True)
            L = PADL + Tin
            cur = h
            curw = 1
            bufs = [PA, PB]
            bi = 0
            while curw < w:
                nxt = bufs[bi]
                bi ^= 1
                n = L - 2 * curw + 1
                tl = dve.do(lambda e, o=nxt[:, 0:n], a=cur[:, 0:n], b=cur[:, curw:curw + n]:
                            e.tensor_tensor(out=o, in0=a, in1=b, op=ALU.add), waits=[tl], sig=True)
                cur = nxt
                curw *= 2
            U = bufs[bi]
            sA = PADL - w // 2
            tl = dve.do(lambda e, o=U[:, 0:Tout], a=cur[:, sA:sA + Tout], c=vcol("cA"):
                        e.tensor_scalar(out=o, in0=a, scalar1=c, scalar2=None, op0=ALU.mult), waits=[tl], sig=True)
            tl = dve.do(lambda e, o=U[:, 0:Tout], a=cur[:, sA + 1:sA + 1 + Tout], c=vcol("cB"):
                        e.scalar_tensor_tensor(out=o, in0=a, scalar=c, in1=o, op0=ALU.mult, op1=ALU.add),
                        waits=[tl], sig=True)
            tl = dve.do(lambda e, o=U[:, 0:8], c=vcol("corr", grp * 8, 8):
                        e.tensor_tensor(out=o, in0=o, in1=c, op=ALU.mult), waits=[tl], sig=True)
            tl = dve.do(lambda e, o=H[:, k, 0:Tout], a=U[:, 0:Tout], b=h[:, PADL:PADL + Tout], iw=1.0 / w:
                        e.scalar_tensor_tensor(out=o, in0=a, scalar=iw, in1=b, op0=ALU.mult, op1=ALU.subtract),
                        waits=[tl], sig=True)
            hf.set_w(tl)
            prev_tl = tl
            if k % 4 == 3:
                for ec in range(4):
                    for (s0, n) in oblocks:
                        ps = ring.next()
                        for c in range(4):
                            t = mm(ps.ap[:, 0:n], WP.ap[:, grp * 4 + c, ec * 128:(ec + 1) * 128], H[:, grp * 4 + c, s0:s0 + n],
                                   c == 0, c == 3, waits=(ps.wwaits() + WP.rwaits() + [tl]) if c == 0 else (), sig=(c == 3))
                        ps.set_w(t)
                        ko = grp * 4 + ec
                        t2 = dve.do(lambda e, o=X[:, ko, s0:s0 + n], a=ps.ap[:, 0:n], sc=vcol(("pscale", j), ko):
                                    e.scalar_tensor_tensor(out=o, in0=a, scalar=sc, in1=o, op0=ALU.mult, op1=ALU.add),
                                    waits=ps.rwaits(), sig=True)
                        ps.add_r(t2)

    def ffn_layer(i):
        T = T_OUT[i]
        H = arena.view(XSZ, (NCH, T), BF16)
        base = XSZ + NCH * T * 2
        P.phase_barrier()
        al = Alloc(arena, base)
        norm_phase(T, ("ffn_g", i), H, al)
        P.phase_barrier()
        al = Alloc(arena, base)
        G = 4
        NG = NF // G
        ACTB = [Slot(al.take((G, T), BF16)) for _ in range(2)]
        SG = Ring([Slot(al.take((512,), F32)) for _ in range(2)])
        WD = Ring([Slot(al.take((G, 128), BF16), sem=P.new_sem(f"wd{q}")) for q in range(6)])
        gu_ring = Ring([Slot((banks[0], banks[1])), Slot((banks[2], banks[3]))])
        dn_ring = bank_ring([4, 5, 6, 7])
        tblocks = blocks(T)

        def gate_up(gi):
            ab = ACTB[gi % 2]
            first = True
            for fi in range(G):
                f = gi * G + fi
                wg = load_wtile(wgate_d[i, f])
                wu = load_wtile(wup_d[i, f])
                for (s, n) in tblocks:
                    pp = gu_ring.next()
                    bg, bu = pp.ap
                    for (bk, w) in ((bg, wg), (bu, wu)):
                        for k in range(NCH):
                            t = mm(bk[:, 0:n], w.ap[:, k * 128:(k + 1) * 128], H[:, k, s:s + n], k == 0, k == NCH - 1,
                                   waits=(pp.wwaits() + w.rwaits()) if k == 0 else (), sig=(k == NCH - 1))
                    pp.set_w(t)
                    sg = SG.next()
                    t2 = act.do(lambda e, o=sg.ap[:, 0:n], a=bg[:, 0:n]: e.activation(out=o, in_=a, func=AF.Silu),
                                waits=pp.rwaits() + sg.wwaits(), sig=True)
                    t3 = dve.do(lambda e, o=ab.ap[:, fi, s:s + n], a=sg.ap[:, 0:n], b=bu[:, 0:n]:
                                e.tensor_tensor(out=o, in0=a, in1=b, op=ALU.mult),
                                waits=[t2] + (ab.wwaits() if first else []), sig=True)
                    first = False
                    sg.set_w(t3)
                    pp.add_r(t3)
                wg.add_r(t)
                wu.add_r(t)
            ab.set_w(t3)

        def down(gi):
            ab = ACTB[gi % 2]
            for dc in range(NCH):
                wd = WD.next()
                dma_in(pool, wd, wd.ap.rearrange("p a b -> p (a b)"), wdown_d[i, gi, dc])
                for (s, n) in tblocks:
                    ps = dn_ring.next()
                    for fi in range(G):
                        t = mm(ps.ap[:, 0:n], wd.ap[:, fi, :], ab.ap[:, fi, s:s + n], fi == 0, fi == G - 1,
                               waits=(ps.wwaits() + wd.rwaits() + ab.rwaits()) if fi == 0 else (), sig=(fi == G - 1))
                    ps.set_w(t)
                    t2 = dve.do(lambda e, o=X[:, dc, s:s + n], a=ps.ap[:, 0:n]:
                                e.tensor_tensor(out=o, in0=a, in1=o, op=ALU.add), waits=ps.rwaits(), sig=True)
                    ps.add_r(t2)
                wd.add_r(t)
            ab.add_r(t)

        gate_up(0)
        for gi in range(NG):
            if gi + 1 < NG:
                gate_up(gi + 1)
            down(gi)

    def ple_layer(i):
        T = T_OUT[i]
        H = arena.view(XSZ, (NCH, T), BF16)
        base = XSZ + NCH * T * 2
        P.phase_barrier()
        al = Alloc(arena, base)
        norm_phase(T, ("ple_g", i), H, al)
        P.phase_barrier()
        al = Alloc(arena, base)
        PT = Slot(al.take((2, T), BF16), sem=P.new_sem("pt"))
        WPP = Slot(al.take((2, D), BF16), sem=P.new_sem("wpp"))
        SG = Ring([Slot(al.take((512,), F32)) for _ in range(2)])
        for c in range(2):
            tk = pool.do(lambda e, c=c: e.dma_start(out=PT.ap[:, c, :], in_=pT[i, c, :, 0:T]), inc=(PT.sem, 16))
        PT.set_w(tk)
        dma_in(pool, WPP, WPP.ap.rearrange("p a b -> p (a b)"), wpp_d[i])
        ring = Ring([Slot((banks[0], banks[1])), Slot((banks[2], banks[3])), Slot((banks[4], banks[5]))])
        for dc in range(NCH):
            w = load_wtile(wpg_d[i, dc])
            for (s, n) in blocks(T):
                pp = ring.next()
                bg, bp = pp.ap
                for k in range(NCH):
                    t = mm(bg[:, 0:n], w.ap[:, k * 128:(k + 1) * 128], H[:, k, s:s + n], k == 0, k == NCH - 1,
                           waits=(pp.wwaits() + w.rwaits()) if k == 0 else ())
                for c in range(2):
                    t = mm(bp[:, 0:n], WPP.ap[:, c, dc * 128:(dc + 1) * 128], PT.ap[:, c, s:s + n], c == 0, c == 1,
                           waits=(PT.rwaits() + WPP.rwaits()) if c == 0 else (), sig=(c == 1))
                pp.set_w(t)
                sg = SG.next()
                t2 = act.do(lambda e, o=sg.ap[:, 0:n], a=bg[:, 0:n], b=vcol(("ple_b", i), dc):
                            e.activation(out=o, in_=a, func=AF.Sigmoid, bias=b, scale=1.0),
                            waits=pp.rwaits() + sg.wwaits(), sig=True)
                t3 = dve.do(lambda e, o=sg.ap[:, 0:n], b=bp[:, 0:n]: e.tensor_tensor(out=o, in0=o, in1=b, op=ALU.mult),
                            waits=[t2], sig=True)
                pp.add_r(t3)
                t4 = dve.do(lambda e, o=X[:, dc, s:s + n], a=sg.ap[:, 0:n]: e.tensor_tensor(out=o, in0=a, in1=o, op=ALU.add),
                            waits=[t3], sig=True)
                sg.set_w(t4)
            w.add_r(t)

    done = False
    for i in range(DEPTH):
        if i % 2 == 0:
            attention_layer(i)
        else:
            pool_layer(i)
        if stop_after is not None and stop_after[0] == i and stop_after[1] in "mnqa":
            done = True
            break
        ffn_layer(i)
        if stop_after == (i, "f"):
            done = True
            break
        ple_layer(i)
        if stop_after == (i, "p"):
            done = True
            break
    if not done:
        P.phase_barrier()
        al = Alloc(arena, XSZ)
        Hdummy = arena.view(XSZ, (NCH, 1024), BF16)
        norm_phase(1024, "final_g", Hdummy, al, out_f32_X=True)
    P.phase_barrier()
    out_sem = P.new_sem("out")
    for k in range(NCH):
        tk = sp.do(lambda e, k=k: e.dma_start(out=outT[k, :, :], in_=X[:, k, 0:T_DUMP]), inc=(out_sem, 16))
    sp.do(lambda e: e.wait_ge(P.sems[out_sem], P.semcount[out_sem]))
    P.replay()
    return nc, stack


def _local_to_global(half, n):
    l = np.arange(n)
    return l if half == 0 else (SEQ - 1 - l)


def _bias_tables(rpb_j, half):
    outs = []
    for (qrow0, krow0, ncz) in ((0, 0, 4), (2, 0, 4), (8, 4, 5)):
        p = np.arange(128)
        c = np.arange(ncz)
        kl_row = krow0 + 2 * c[:, None] + p[None, :] // 64
        kl_col = np.broadcast_to(p[None, :] % 64, kl_row.shape)
        q = np.arange(128)
        ql_row = qrow0 + q // 64
        ql_col = q % 64
        if half == 0:
            rk, ck, rq, cq = kl_row, kl_col, ql_row, ql_col
        else:
            rk, ck, rq, cq = 31 - kl_row, 63 - kl_col, 31 - ql_row, 63 - ql_col
        rk = rk[:, :, None]
        ck = ck[:, :, None]
        rq = rq[None, None, :]
        cq = cq[None, None, :]
        r_start = np.clip(rq - 4, 0, 24)
        c_start = np.clip(cq - 8, 0, 48)
        valid = (rk >= r_start) & (rk < r_start + 8) & (ck >= c_start) & (ck < c_start + 16)
        dr = np.clip(rk - rq + 7, 0, 14)
        dc = np.clip(ck - cq, -15, 15) + 15
        dr, dc, valid = np.broadcast_arrays(dr, dc, valid)
        gathered = rpb_j[:, dr, dc]
        tab = np.where(valid[None], gathered, np.float32(NEG)).astype(np.float32)
        tab = tab.reshape(16, 4, ncz, 128, 128).transpose(0, 3, 2, 1, 4)
        outs.append(np.ascontiguousarray(tab).reshape(16, 128, 4 * ncz * 128))
    return outs


def _fm(v):
    return np.ascontiguousarray(v.reshape(NCH, 128).T)


def _wtiles(w, ncol_chunks):
    N = w.shape[1]
    t = w.reshape(NCH, 128, N // 128, 128).transpose(2, 1, 0, 3)
    return np.ascontiguousarray(t).reshape(N // 128, 128, NCH * 128)


def prepare_inputs(inp):
    f32 = np.float32
    g = {k: np.asarray(v, dtype=f32) for k, v in inp.items()}
    shared = {}
    shared["wqkv"] = np.stack([_wtiles(g["w_qkv"][j], 48).reshape(3, 16, 128, 2048).transpose(1, 0, 2, 3)
                               for j in range(2)])
    shared["wo"] = np.ascontiguousarray(
        g["w_o"].reshape(2, 8, 2, 128, D).transpose(0, 1, 3, 2, 4))
    shared["wpool"] = np.ascontiguousarray(
        g["w_pool"].reshape(2, 4, 4, 128, 512).transpose(0, 3, 1, 2, 4)).reshape(2, 128, 16 * 512)
    shared["wgate"] = np.stack([_wtiles(g["w_gate"][i], NF) for i in range(DEPTH)])
    shared["wup"] = np.stack([_wtiles(g["w_up"][i], NF) for i in range(DEPTH)])
    shared["wdown"] = np.ascontiguousarray(
        g["w_down"].reshape(DEPTH, 11, 4, 128, NCH, 128).transpose(0, 1, 4, 3, 2, 5)).reshape(DEPTH, 11, 16, 128, 512)
    shared["wpg"] = np.stack([_wtiles(g["w_ple_gate"][i], 16) for i in range(DEPTH)])
    shared["wpp"] = np.ascontiguousarray(
        g["w_ple_proj"].reshape(DEPTH, 2, 128, D).transpose(0, 2, 1, 3)).reshape(DEPTH, 128, 2 * D)
    shared["ident"] = np.eye(128, dtype=f32)
    shared["bv"] = np.ascontiguousarray(np.broadcast_to(g["b_qkv"][:, None, 2 * D:3 * D], (2, 128, D)))
    tabs = {h: [_bias_tables(g["rpb"][j], h) for j in range(2)] for h in (0, 1)}
    vec_common = np.zeros((128, NVEC), f32)
    for i in range(DEPTH):
        j = i // 2
        mg = g["attn_norm_g"][j] if i % 2 == 0 else g["pool_norm_g"][j]
        vec_common[:, VC[("mix_g", i)]:VC[("mix_g", i)] + 16] = _fm(mg)
        vec_common[:, VC[("ffn_g", i)]:VC[("ffn_g", i)] + 16] = _fm(g["ffn_norm_g"][i])
        vec_common[:, VC[("ple_g", i)]:VC[("ple_g", i)] + 16] = _fm(g["ple_norm_g"][i])
        vec_common[:, VC[("ple_b", i)]:VC[("ple_b", i)] + 16] = _fm(g["b_ple_gate"][i])
    for j in range(2):
        vec_common[:, VC[("bq", j)]:VC[("bq", j)] + 16] = _fm(g["b_qkv"][j, 0:D])
        vec_common[:, VC[("bk", j)]:VC[("bk", j)] + 16] = _fm(g["b_qkv"][j, D:2 * D])
        vec_common[:, VC[("pscale", j)]:VC[("pscale", j)] + 16] = _fm(g["pool_scale"][j])
    vec_common[:, VC["final_g"]:VC["final_g"] + 16] = _fm(g["final_norm_g"])
    in_maps = []
    for core in range(8):
        b, half = core // 2, core % 2
        m = dict(shared)
        idx = _local_to_global(half, TX0)
        m["xT"] = np.ascontiguousarray(g["x"][b][idx].T).reshape(NCH, 128, TX0)
        idp = _local_to_global(half, TX)
        m["pT"] = np.ascontiguousarray(g["p"][:, b][:, idp].transpose(0, 2, 1)).reshape(DEPTH, 2, 128, TX)
        v = vec_common.copy()
        v[:, VC["cA"]] = 1.0 if half == 0 else 0.0
        v[:, VC["cB"]] = 0.0 if half == 0 else 1.0
        for gi, w in enumerate(POOL_W):
            t = np.arange(8)
            cnt = np.minimum(w, t + w // 2) if half == 0 else np.minimum(w, t + 1 + w // 2)
            v[:, VC["corr"] + gi * 8: VC["corr"] + gi * 8 + 8] = (w / cnt).astype(f32)[None, :]
        v[:, VC["eps"]] = EPS
        for hh in range(4):
            v[32 * hh:32 * hh + 32, VC["hmask"] + hh] = 1.0
        m["vecs"] = v
        for j in range(2):
            pass
        m["tab0"] = np.stack([tabs[half][j][0] for j in range(2)])
        m["tab1"] = np.stack([tabs[half][j][1] for j in range(2)])
        m["tab2"] = np.stack([tabs[half][j][2] for j in range(2)])
        in_maps.append(m)
    return in_maps


def assemble(results, T):
    out = np.zeros((4, SEQ, D), np.float32)
    for core in range(8):
        b, half = core // 2, core % 2
        o = results[core]["outT"].reshape(D, -1)[:, :T].T
        idx = _local_to_global(half, T)
        out[b, idx] = o
    return out


def kernel(**inputs):
    in_maps = prepare_inputs(inputs)
    nc, stack = build_program(None)
    with stack:
        res = run_bass_kernel_spmd(nc, in_maps, core_ids=list(range(8)))
    return assemble(res.results, 1024)
```

```python
import numpy as np
from contextlib import ExitStack
import concourse.bass as bass
import concourse.mybir as mybir
from concourse.bass_utils import run_bass_kernel_spmd

F32 = mybir.dt.float32
BF16 = mybir.dt.bfloat16
AF = mybir.ActivationFunctionType
ALU = mybir.AluOpType

D = 2048
NCH = 16
DFF = 5632
NF = 44
DEPTH = 4
SEQ = 2048
GW = 64
EPS = 1e-6
NEG = -30000.0
POOL_W = (2, 4, 8, 16)

T_KV = {0: 1664, 2: 1344}
T_OUT = {0: 1352, 1: 1344, 2: 1032, 3: 1024}
T_IN = {0: 1352, 1: 1352, 2: 1344, 3: 1032}
TX = 1352
TX0 = 1664
ARENA_BYTES = 209920

VC = {}
_c = 0
for _i in range(DEPTH):
    for _nm in ("mix_g", "ffn_g", "ple_g", "ple_b"):
        VC[(_nm, _i)] = _c
        _c += 16
for _j in range(2):
    for _nm in ("bq", "bk", "pscale"):
        VC[(_nm, _j)] = _c
        _c += 16
VC["final_g"] = _c
_c += 16
VC["cA"] = _c
VC["cB"] = _c + 1
_c += 2
VC["corr"] = _c
_c += 32
VC["eps"] = _c
_c += 1
VC["hmask"] = _c
_c += 4
NVEC = _c


def blocks(T, maxn=512):
    nb = -(-T // maxn)
    base = -(-T // nb)
    base = -(-base // 8) * 8
    out = []
    s = 0
    while s < T:
        n = min(base, T - s)
        out.append((s, n))
        s += n
    return out


class Slot:
    def __init__(self, ap=None, **kw):
        self.ap = ap
        self.w = []
        self.r = []
        self.__dict__.update(kw)

    def wwaits(self):
        return list(self.r) + list(self.w)

    def set_w(self, *toks):
        self.w = [t for t in toks if t is not None]
        self.r = []

    def add_w(self, tok):
        self.w.append(tok)

    def rwaits(self):
        return list(self.w)

    def add_r(self, tok):
        if tok is not None:
            self.r.append(tok)


class Ring:
    def __init__(self, slots):
        self.slots = slots
        self.i = 0

    def next(self):
        s = self.slots[self.i % len(self.slots)]
        self.i += 1
        return s


class Eng:
    def __init__(self, prog, name, semi):
        self.prog = prog
        self.name = name
        self.semi = semi
        self.ops = []
        self.waited = {}

    def do(self, fn, waits=(), sig=False, inc=None, nobar=False):
        ws = []
        allw = list(waits)
        if not nobar:
            allw += self.prog.barrier
        for t in allw:
            if t is None:
                continue
            s, v = t
            if self.waited.get(s, 0) >= v:
                continue
            self.waited[s] = v
            ws.append((s, v))
        op = {"fn": fn, "ws": ws, "inc": None}
        self.ops.append(op)
        if inc is not None:
            return self._inc(op, inc)
        if sig:
            return self._inc(op, (self.semi, 1))
        return None

    def _inc(self, op, inc):
        assert op["inc"] is None
        op["inc"] = inc
        self.prog.semcount[inc[0]] += inc[1]
        op["tok"] = (inc[0], self.prog.semcount[inc[0]])
        return op["tok"]

    def last_token(self):
        if not self.ops:
            return None
        op = self.ops[-1]
        if op["inc"] is None:
            return self._inc(op, (self.semi, 1))
        return op["tok"]


class Prog:
    def __init__(self, nc, stack):
        self.nc = nc
        self.stack = stack
        self.sems = []
        self.semnames = {}
        self.semcount = []
        self.barrier = []
        self.pe = Eng(self, "pe", self.new_sem("pe"))
        self.act = Eng(self, "act", self.new_sem("act"))
        self.dve = Eng(self, "dve", self.new_sem("dve"))
        self.sp = Eng(self, "sp", None)
        self.pool = Eng(self, "pool", None)

    def new_sem(self, name):
        if name in self.semnames:
            return self.semnames[name]
        self.semnames[name] = len(self.sems)
        s = self.stack.enter_context(self.nc.semaphore(name))
        self.sems.append(s)
        self.semcount.append(0)
        return len(self.sems) - 1

    def phase_barrier(self):
        toks = [e.last_token() for e in (self.pe, self.act, self.dve)]
        self.barrier = [t for t in toks if t is not None]

    def replay(self):
        sems = self.sems

        def run(E, e):
            for op in E.ops:
                for (s, v) in op["ws"]:
                    e.wait_ge(sems[s], v)
                ins = op["fn"](e)
                if op["inc"] is not None:
                    ins.then_inc(sems[op["inc"][0]], op["inc"][1])

        with self.nc.Block() as block:
            @block.tensor
            def _(e):
                run(self.pe, e)

            @block.scalar
            def _(e):
                run(self.act, e)

            @block.vector
            def _(e):
                run(self.dve, e)

            @block.sync
            def _(e):
                run(self.sp, e)

            @block.gpsimd
            def _(e):
                run(self.pool, e)


class Arena:
    def __init__(self, ap, nbytes):
        self.ap = ap
        self.nbytes = nbytes

    def view(self, off, shape, dtype):
        n = int(np.prod(shape))
        assert off % 4 == 0
        if dtype == BF16:
            assert off + 2 * n <= self.nbytes, (off, shape)
            v = self.ap[:, off // 2: off // 2 + n]
        else:
            assert off + 4 * n <= self.nbytes, (off, shape)
            v = self.ap[:, off // 2: off // 2 + 2 * n].bitcast(F32)
        if len(shape) == 2:
            v = v.rearrange("p (a b) -> p a b", b=shape[1])
        elif len(shape) == 3:
            v = v.rearrange("p (a b c) -> p a b c", b=shape[1], c=shape[2])
        return v


class Alloc:
    def __init__(self, arena, base):
        self.arena = arena
        self.off = base

    def take(self, shape, dtype):
        n = int(np.prod(shape)) * (2 if dtype == BF16 else 4)
        n = -(-n // 64) * 64
        v = self.arena.view(self.off, shape, dtype)
        self.off += n
        assert self.off <= self.arena.nbytes - 4 * 4096, ("arena overflow", self.off)
        return v


def build_program(stop_after=None):
    nc = bass.Bass("TRN2", target_bir_lowering=False)
    stack = ExitStack()
    dt = nc.dram_tensor
    xT = dt("xT", [NCH, 128, TX0], F32, kind="ExternalInput").ap()
    pT = dt("pT", [DEPTH, 2, 128, TX], F32, kind="ExternalInput").ap()
    vecs_d = dt("vecs", [128, NVEC], F32, kind="ExternalInput").ap()
    bv_d = dt("bv", [2, 128, D], F32, kind="ExternalInput").ap()
    ident_d = dt("ident", [128, 128], F32, kind="ExternalInput").ap()
    wqkv_d = dt("wqkv", [2, 16, 3, 128, 2048], F32, kind="ExternalInput").ap()
    wo_d = dt("wo", [2, 8, 128, 2, D], F32, kind="ExternalInput").ap()
    tab0_d = dt("tab0", [2, 16, 128, 4 * 4 * 128], F32, kind="ExternalInput").ap()
    tab1_d = dt("tab1", [2, 16, 128, 4 * 4 * 128], F32, kind="ExternalInput").ap()
    tab2_d = dt("tab2", [2, 16, 128, 4 * 5 * 128], F32, kind="ExternalInput").ap()
    wpool_d = dt("wpool", [2, 128, 16 * 512], F32, kind="ExternalInput").ap()
    wgate_d = dt("wgate", [DEPTH, NF, 128, 2048], F32, kind="ExternalInput").ap()
    wup_d = dt("wup", [DEPTH, NF, 128, 2048], F32, kind="ExternalInput").ap()
    wdown_d = dt("wdown", [DEPTH, 11, 16, 128, 4 * 128], F32, kind="ExternalInput").ap()
    wpg_d = dt("wpg", [DEPTH, 16, 128, 2048], F32, kind="ExternalInput").ap()
    wpp_d = dt("wpp", [DEPTH, 128, 2 * D], F32, kind="ExternalInput").ap()
    T_DUMP = 1024 if stop_after is None else TX
    outT = dt("outT", [NCH, 128, T_DUMP], F32, kind="ExternalOutput").ap()

    P = Prog(nc, stack)
    pe, act, dve, sp, pool = P.pe, P.act, P.dve, P.sp, P.pool
    arena_t = stack.enter_context(nc.sbuf_tensor("arena", [128, ARENA_BYTES // 2], BF16))
    arena = Arena(arena_t, ARENA_BYTES)
    vecs = stack.enter_context(nc.sbuf_tensor("vecs_sb", [128, NVEC], F32))
    ones = stack.enter_context(nc.sbuf_tensor("ones_sb", [128, 128], BF16))
    ident = stack.enter_context(nc.sbuf_tensor("ident_sb", [128, 128], BF16))
    bqs = stack.enter_context(nc.sbuf_tensor("bqs_sb", [128, 16], F32))
    sbig = stack.enter_context(nc.psum_tensor("sbig", [128, 2048], F32))
    banks = [sbig[:, i * 512:(i + 1) * 512] for i in range(4)]
    banks += [stack.enter_context(nc.psum_tensor(f"bank{i}", [128, 512], F32))[:, :] for i in range(4, 8)]
    sview = sbig[:, :].rearrange("p (a s b) -> p a s b", s=4, b=128)

    XSZ = NCH * TX * 4
    X = arena.view(0, (NCH, TX), F32)
    NWT = 4
    WT_BASE = ARENA_BYTES - NWT * 4096
    wt_slots = [Slot(arena.view(WT_BASE + i * 4096, (2048,), BF16), sem=P.new_sem(f"wt{i}")) for i in range(NWT)]
    wt_ring = Ring(wt_slots)

    def vcol(key, k=0, n=1):
        c = VC[key] + k
        return vecs[:, c:c + n]

    def dma_in(E, slot, out_ap, in_ap, nobar=False, extra_waits=()):
        tok = E.do(lambda e, o=out_ap, i=in_ap: e.dma_start(out=o, in_=i),
                   waits=slot.wwaits() + list(extra_waits), inc=(slot.sem, 16), nobar=nobar)
        slot.set_w(tok)
        return tok

    def load_wtile(src_ap):
        s = wt_ring.next()
        dma_in(pool, s, s.ap, src_ap, nobar=True)
        return s

    def mm(out, lhsT, rhs, start, stop, waits=(), sig=False, tp=None):
        if tp is None:
            fn = lambda e: e.matmul(out, lhsT=lhsT, rhs=rhs, start=start, stop=stop)
        else:
            fn = lambda e: e.matmul(out, lhsT=lhsT, rhs=rhs, start=start, stop=stop, tile_position=tp)
        return pe.do(fn, waits=waits, sig=sig)

    def bank_ring(idxs):
        return Ring([Slot(banks[i]) for i in idxs])

    vec_slot = Slot(sem=P.new_sem("vecs"))
    dma_in(sp, vec_slot, vecs[:, :], vecs_d[:, :])
    x_slot = Slot(sem=P.new_sem("xload"))
    xtoks = []
    for k in range(NCH):
        xtoks.append(sp.do(lambda e, k=k: e.dma_start(out=X[:, k, :], in_=xT[k, :, 0:TX]),
                           inc=(x_slot.sem, 16)))
    x_ready = [xtoks[-1]]
    ones_tok = dve.do(lambda e: e.memset(ones[:, :], 1.0), sig=True)
    id_slot = Slot(sem=P.new_sem("ident"))
    dma_in(pool, id_slot, ident[:, :], ident_d[:, :])
    const_ready = [ones_tok] + vec_slot.rwaits() + id_slot.rwaits()

    def norm_phase(T, gkey, H, al, src=None, tok_off=0, keep_rstd=False, out_f32_X=False, extra_waits=(), ring=None,
                   blk_toks=None):
        if src is None:
            src = lambda k, s, n: X[:, k, s:s + n]
        SQ = Ring([Slot(al.take((512,), BF16)) for _ in range(3)])
        TMP = Slot(al.take((512,), F32))
        RSTD = al.take((T,), F32)
        if ring is None:
            ring = bank_ring([6, 7])
        ew = list(extra_waits) + const_ready + x_ready
        for (s, n) in blocks(T):
            ps = ring.next()
            for k in range(NCH):
                sq = SQ.next()
                t = act.do(lambda e, o=sq.ap[:, 0:n], i=src(k, s, n): e.activation(out=o, in_=i, func=AF.Square),
                           waits=sq.wwaits() + ew, sig=True)
                sq.set_w(t)
                t = mm(ps.ap[:, 0:n], ones[:, :], sq.ap[:, 0:n], k == 0, k == NCH - 1,
                       waits=sq.rwaits() + (ps.wwaits() if k == 0 else []) + ew, sig=True)
                sq.add_r(t)
            ps.set_w(t)
            t = act.do(lambda e, o=TMP.ap[:, 0:n], i=ps.ap[:, 0:n]: e.activation(
                out=o, in_=i, func=AF.Sqrt, bias=vcol("eps"), scale=1.0 / D),
                waits=ps.rwaits() + TMP.wwaits(), sig=True)
            ps.add_r(t)
            TMP.set_w(t)
            t = dve.do(lambda e, o=RSTD[:, s:s + n], i=TMP.ap[:, 0:n]: e.reciprocal(out=o, in_=i),
                       waits=TMP.rwaits(), sig=True)
            TMP.add_r(t)
            if H is not None:
                for k in range(NCH):
                    o = X[:, k, s:s + n] if out_f32_X else H[:, k, tok_off + s: tok_off + s + n]
                    t2 = dve.do(lambda e, o=o, i=src(k, s, n), g=vcol(gkey, k), r=RSTD[:, s:s + n]:
                                e.scalar_tensor_tensor(out=o, in0=i, scalar=g, in1=r, op0=ALU.mult, op1=ALU.mult),
                                waits=[t] + ew, sig=(k == NCH - 1))
                if blk_toks is not None:
                    blk_toks.append(t2)
        return RSTD, t

    def attention_layer(i):
        j = i // 2
        Tkv, Tq = T_KV[i], T_OUT[i]
        H = arena.view(XSZ, (NCH, Tkv), BF16)
        base = XSZ + NCH * Tkv * 2
        P.phase_barrier()
        al = Alloc(arena, base)
        if i == 0:
            norm_phase(TX, ("mix_g", i), H, al)
            nh = Tkv - TX
            XH = al.take((NCH, nh), F32)
            xh_slot = Slot(sem=P.new_sem("xh"))
            for k in range(NCH):
                tk = sp.do(lambda e, k=k: e.dma_start(out=XH[:, k, :], in_=xT[k, :, TX:Tkv]),
                           inc=(xh_slot.sem, 16))
            al2 = Alloc(arena, al.off)
            norm_phase(nh, ("mix_g", i), H, al2, src=lambda k, s, n: XH[:, k, s:s + n], tok_off=TX,
                       extra_waits=[tk])
        else:
            norm_phase(Tkv, ("mix_g", i), H, al)
        if stop_after == (i, "n"):
            return
        P.phase_barrier()
        al = Alloc(arena, base)
        nkc = Tkv // 128 if Tkv % 128 == 0 else Tkv // 128 + 1
        scale = 32 ** -0.5
        QKV = []
        for b in range(2):
            QKV.append(dict(Q=Slot(al.take((Tq,), BF16)), K=Slot(al.take((Tkv,), BF16)),
                            V=Slot(al.take((nkc, 128), BF16))))
        TB = [Slot(al.take((4 * 512,), BF16), sem=P.new_sem("tb0")),
              Slot(al.take((4 * 512,), BF16), sem=P.new_sem("tb1")),
              Slot(al.take((5 * 512,), BF16), sem=P.new_sem("tb2"))]
        tbq = dve.do(lambda e: e.tensor_scalar(out=bqs[:, :], in0=vcol(("bq", j), 0, 16), scalar1=scale, scalar2=None,
                                               op0=ALU.mult), waits=const_ready, sig=True)
        tabs_d = [tab0_d, tab1_d, tab2_d]
        OT = [Slot(al.take((2, Tq), BF16))]
        WO = Slot(al.take((2, D), BF16), sem=P.new_sem("wo"))
        PR = Ring([Slot(al.take((4, 128), BF16)) for _ in range(3)])
        RC = Ring([Slot(al.take((128,), F32)) for _ in range(2)])
        BVB = Ring([Slot(al.take((128,), F32), sem=P.new_sem(f"bvb{q}")) for q in range(2)])
        s_ring = bank_ring([0, 1])
        o_ring = Ring([Slot((banks[2], banks[3])), Slot((banks[4], banks[5]))])
        g_ring = bank_ring([6, 7])
        wo_ring = Ring(s_ring.slots + g_ring.slots)
        QB = Ring([Slot(al.take((4, 128), BF16)) for _ in range(2)])
        for qbs in QB.slots:
            qbs.set_w(dve.do(lambda e, o=qbs.ap: e.memset(o, 0.0), sig=True))
        qblocks = blocks(Tq)
        kblocks = blocks(Tkv)
        wtiles = {}
        bvbs = {}

        def dma_w(g):
            wtiles[g] = [load_wtile(wqkv_d[j, g, s]) for s in range(3)]
            sl = BVB.next()
            dma_in(sp, sl, sl.ap, bv_d[j, :, g * 128:(g + 1) * 128])
            bvbs[g] = sl

        def dma_tab(g):
            for t in range(3):
                sl = TB[t]
                dma_in(pool, sl, sl.ap, tabs_d[t][j, g])

        def qkv(g):
            buf = QKV[g % 2]
            wq, wk, wv = wtiles[g]
            for (dst, w, bkey, blks) in ((buf["Q"], wq, None, qblocks), (buf["K"], wk, "bk", kblocks)):
                first = True
                toks = []
                for (s, n) in blks:
                    ps = g_ring.next()
                    for k in range(NCH):
                        t = mm(ps.ap[:, 0:n], w.ap[:, k * 128:(k + 1) * 128], H[:, k, s:s + n], k == 0, k == NCH - 1,
                               waits=(ps.wwaits() + w.rwaits()) if k == 0 else (), sig=(k == NCH - 1))
                    ps.set_w(t)
                    bb = bqs[:, g:g + 1] if bkey is None else vcol((bkey, j), g)
                    sc = scale if bkey is None else 1.0
                    t2 = act.do(lambda e, o=dst.ap[:, s:s + n], i=ps.ap[:, 0:n], b=bb, sc=sc:
                                e.activation(out=o, in_=i, func=AF.Identity, bias=b, scale=sc),
                                waits=ps.rwaits() + (dst.wwaits() if first else []) + [tbq], sig=True)
                    first = False
                    ps.add_r(t2)
                    toks.append(t2)
                w.add_r(t)
                dst.set_w(toks[-1])
            dstv = buf["V"]
            bvb = bvbs[g]
            first = True
            for c0 in range(0, nkc, 4):
                ps = g_ring.next()
                ncs = min(4, nkc - c0)
                for cc in range(ncs):
                    c = c0 + cc
                    nk = min(128, Tkv - c * 128)
                    for k in range(NCH):
                        t = mm(ps.ap[0:nk, cc * 128:(cc + 1) * 128], H[:, k, c * 128:c * 128 + nk],
                               wv.ap[:, k * 128:(k + 1) * 128], k == 0, k == NCH - 1,
                               waits=(ps.wwaits() + wv.rwaits()) if (k == 0 and cc == 0) else (),
                               sig=(k == NCH - 1 and cc == ncs - 1))
                ps.set_w(t)
                for cc in range(ncs):
                    c = c0 + cc
                    nk = min(128, Tkv - c * 128)
                    t2 = dve.do(lambda e, o=dstv.ap[0:nk, c, :], a=ps.ap[0:nk, cc * 128:(cc + 1) * 128], b=bvb.ap[0:nk, :]:
                                e.tensor_tensor(out=o, in0=a, in1=b, op=ALU.add),
                                waits=ps.rwaits() + bvb.rwaits() + (dstv.wwaits() if first else []), sig=(cc == ncs - 1))
                    first = False
                ps.add_r(t2)
            wv.add_r(t)
            bvb.add_r(t2)
            dstv.set_w(t2)

        def attend(g):
            buf = QKV[g % 2]
            Qs, Ks, Vs = buf["Q"], buf["K"], buf["V"]
            ot = OT[0]
            gi = g % 2
            npair = -(-Tq // 128)
            last_pv = None
            q_last_tok = [None]
            for pj in range(npair):
                q0 = pj * 128
                nq = min(128, Tq - q0)
                if pj < 2:
                    tb, cs, ncz = TB[pj], 0, 4
                else:
                    tb, cs, ncz = TB[2], pj - 2, 5
                ob = o_ring.next()
                chunks = []
                for c in range(ncz):
                    kc = cs + c
                    nk = min(128, Tkv - kc * 128)
                    assert nk in (64, 128)
                    chunks.append((c, kc, nk))
                qb = QB.next()
                qb_last = [None]
                for hh in range(4):
                    tq = dve.do(lambda e, o=qb.ap[:, hh, 0:nq], i=Qs.ap[:, q0:q0 + nq], m=vcol("hmask", hh):
                                e.tensor_scalar(out=o, in0=i, scalar1=m, scalar2=None, op0=ALU.mult),
                                waits=(qb.wwaits() + Qs.rwaits()) if hh == 0 else (), sig=(hh == 3))
                qb.set_w(tq)
                q_last_tok[0] = tq

                def issue_s(c, kc, nk):
                    sb = s_ring.next()
                    sbv = sb.ap.rearrange("p (a b) -> p a b", b=128)[0:nk, :, 0:nq]
                    t = mm(sb.ap[0:nk, :], Ks.ap[:, kc * 128:kc * 128 + nk], qb.ap.rearrange("p a b -> p (a b)"), True, False,
                           waits=sb.wwaits() + qb.rwaits() + Ks.rwaits())
                    t = mm(sb.ap[0:nk, :], ident[0:nk, 0:nk], tb.ap[0:nk, c * 512:(c + 1) * 512], False, True,
                           waits=tb.rwaits(), sig=True)
                    qb_last[0] = t
                    sb.set_w(t)
                    pr = PR.next()
                    t2 = act.do(lambda e, o=pr.ap[0:nk, :, 0:nq], i=sbv: e.activation(out=o, in_=i, func=AF.Exp),
                                waits=sb.rwaits() + pr.wwaits(), sig=True)
                    sb.add_r(t2)
                    pr.set_w(t2)
                    return pr

                def issue_pv(c, kc, nk, pr):
                    nonlocal last_pv
                    for hh in range(4):
                        w = (pr.rwaits() + Vs.rwaits() + (ob.wwaits() if c == 0 else [])) if hh == 0 else ()
                        mm(ob.ap[0][32 * hh:32 * hh + 32, 0:nq], Vs.ap[0:nk, kc, 32 * hh:32 * hh + 32], pr.ap[0:nk, hh, 0:nq],
                           c == 0, c == ncz - 1, waits=w, tp=(0, 32 * hh))
                        t = mm(ob.ap[1][32 * hh:32 * hh + 32, 0:nq], ones[0:nk, 0:32], pr.ap[0:nk, hh, 0:nq],
                               c == 0, c == ncz - 1, sig=(hh == 3), tp=(0, 32 * hh))
                    pr.add_r(t)
                    last_pv = t
                    return t

                prs = {}
                prs[0] = issue_s(*chunks[0])
                for c in range(ncz):
                    if c + 1 < ncz:
                        prs[c + 1] = issue_s(*chunks[c + 1])
                    t = issue_pv(*chunks[c], prs[c])
                ob.set_w(t)
                qb.add_r(qb_last[0])
                rc = RC.next()
                t1 = dve.do(lambda e, o=rc.ap[:, 0:nq], i=ob.ap[1][:, 0:nq]: e.reciprocal(out=o, in_=i),
                            waits=ob.rwaits() + rc.wwaits(), sig=True)
                t2 = dve.do(lambda e, o=ot.ap[:, gi, q0:q0 + nq], a=ob.ap[0][:, 0:nq], b=rc.ap[:, 0:nq]:
                            e.tensor_tensor(out=o, in0=a, in1=b, op=ALU.mult),
                            waits=[t1] + (ot.wwaits() if (pj == 0 and gi == 0) else []), sig=True)
                rc.set_w(t2)
                ob.add_r(t2)
            Qs.add_r(q_last_tok[0])
            Ks.add_r(last_pv)
            Vs.add_r(last_pv)
            if gi == 0:
                ot.set_w(t2)
            else:
                ot.add_w(t2)

        def wo_dma(s):
            flat = WO.ap.rearrange("p a b -> p (a b)")
            dma_in(pool, WO, flat, wo_d[j, s].rearrange("p a b -> p (a b)"))

        def wo_set(s):
            ot = OT[0]
            for dc in range(NCH):
                for (s0, n) in qblocks:
                    ps = wo_ring.next()
                    for gi in range(2):
                        t = mm(ps.ap[:, 0:n], WO.ap[:, gi, dc * 128:(dc + 1) * 128], ot.ap[:, gi, s0:s0 + n], gi == 0, gi == 1,
                               waits=(ps.wwaits() + WO.rwaits() + ot.rwaits()) if gi == 0 else (), sig=(gi == 1))
                    ps.set_w(t)
                    t2 = dve.do(lambda e, o=X[:, dc, s0:s0 + n], a=ps.ap[:, 0:n]:
                                e.tensor_tensor(out=o, in0=a, in1=o, op=ALU.add),
                                waits=ps.rwaits() + x_ready, sig=True)
                    ps.add_r(t2)
            WO.add_r(t)
            ot.add_r(t)

        def attend_body(g):
            attend(g)
            tok = pe.last_token()
            for t in range(3):
                TB[t].add_r(tok)

        dma_w(0)
        dma_tab(0)
        qkv(0)
        if stop_after == (i, "q"):
            return
        if stop_after == (i, "a"):
            attend_body(0)
            return
        if 1 < 16:
            dma_w(1)
        for g in range(16):
            if g + 1 < 16:
                qkv(g + 1)
            if g + 2 < 16:
                dma_w(g + 2)
            if g % 2 == 1:
                wo_dma(g // 2)
            attend_body(g)
            if g + 1 < 16:
                dma_tab(g + 1)
            if g % 2 == 1:
                wo_set(g // 2)

    def pool_layer(i):
        j = i // 2
        Tin, Tout = T_IN[i], T_OUT[i]
        H = arena.view(XSZ, (NCH, Tin), BF16)
        base = XSZ + NCH * Tin * 2
        P.phase_barrier()
        al = Alloc(arena, base)
        WP = Slot(al.take((16, 512), BF16), sem=P.new_sem("wp"))
        dma_in(pool, WP, WP.ap.rearrange("p a b -> p (a b)"), wpool_d[j])
        RSTD, rtok = norm_phase(Tin, None, None, al)
        PADL = 8
        HF = [Slot(al.take((PADL + Tin,), F32)) for _ in range(2)]
        PA = al.take((PADL + Tin,), F32)
        PB = al.take((PADL + Tin,), F32)
        for hf in HF:
            t = dve.do(lambda e, o=hf.ap[:, 0:PADL]: e.memset(o, 0.0), sig=True)
            hf.set_w(t)
        ring = bank_ring([0, 1, 2, 3])
        oblocks = blocks(Tout)
        prev_tl = rtok
        for k in range(NCH):
            grp = k // 4
            w = POOL_W[grp]
            hf = HF[k % 2]
            h = hf.ap
            tl = dve.do(lambda e, o=h[:, PADL:PADL + Tin], x=X[:, k, 0:Tin], g=vcol(("mix_g", i), k), r=RSTD[:, 0:Tin]:
                        e.scalar_tensor_tensor(out=o, in0=x, scalar=g, in1=r, op0=ALU.mult, op1=ALU.mult),
                        waits=hf.wwaits() + const_ready + [prev_tl], sig=True)
            L = PADL + Tin
            cur = h
            curw = 1
            bufs = [PA, PB]
            bi = 0
            while curw < w:
                nxt = bufs[bi]
                bi ^= 1
                n = L - 2 * curw + 1
                tl = dve.do(lambda e, o=nxt[:, 0:n], a=cur[:, 0:n], b=cur[:, curw:curw + n]:
                            e.tensor_tensor(out=o, in0=a, in1=b, op=ALU.add), waits=[tl], sig=True)
                cur = nxt
                curw *= 2
            U = bufs[bi]
            sA = PADL - w // 2
            tl = dve.do(lambda e, o=U[:, 0:Tout], a=cur[:, sA:sA + Tout], c=vcol("cA"):
                        e.tensor_scalar(out=o, in0=a, scalar1=c, scalar2=None, op0=ALU.mult), waits=[tl], sig=True)
            tl = dve.do(lambda e, o=U[:, 0:Tout], a=cur[:, sA + 1:sA + 1 + Tout], c=vcol("cB"):
                        e.scalar_tensor_tensor(out=o, in0=a, scalar=c, in1=o, op0=ALU.mult, op1=ALU.add),
                        waits=[tl], sig=True)
            tl = dve.do(lambda e, o=U[:, 0:8], c=vcol("corr", grp * 8, 8):
                        e.tensor_tensor(out=o, in0=o, in1=c, op=ALU.mult), waits=[tl], sig=True)
            tl = dve.do(lambda e, o=H[:, k, 0:Tout], a=U[:, 0:Tout], b=h[:, PADL:PADL + Tout], iw=1.0 / w:
                        e.scalar_tensor_tensor(out=o, in0=a, scalar=iw, in1=b, op0=ALU.mult, op1=ALU.subtract),
                        waits=[tl], sig=True)
            hf.set_w(tl)
            prev_tl = tl
            if k % 4 == 3:
                for ec in range(4):
                    for (s0, n) in oblocks:
                        ps = ring.next()
                        for c in range(4):
                            t = mm(ps.ap[:, 0:n], WP.ap[:, grp * 4 + c, ec * 128:(ec + 1) * 128], H[:, grp * 4 + c, s0:s0 + n],
                                   c == 0, c == 3, waits=(ps.wwaits() + WP.rwaits() + [tl]) if c == 0 else (), sig=(c == 3))
                        ps.set_w(t)
                        ko = grp * 4 + ec
                        t2 = dve.do(lambda e, o=X[:, ko, s0:s0 + n], a=ps.ap[:, 0:n], sc=vcol(("pscale", j), ko):
                                    e.scalar_tensor_tensor(out=o, in0=a, scalar=sc, in1=o, op0=ALU.mult, op1=ALU.add),
                                    waits=ps.rwaits(), sig=True)
                        ps.add_r(t2)

    def ffn_layer(i):
        T = T_OUT[i]
        H = arena.view(XSZ, (NCH, T), BF16)
        base = XSZ + NCH * T * 2
        P.phase_barrier()
        al = Alloc(arena, base)
        G = 4
        NG = NF // G
        ACTB = [Slot(al.take((G, T), BF16)) for _ in range(2)]
        SG = Ring([Slot(al.take((512,), F32)) for _ in range(2)])
        WD = Ring([Slot(al.take((G, 128), BF16), sem=P.new_sem(f"wd{q}")) for q in range(6)])
        gu_ring = Ring([Slot((banks[0], banks[1])), Slot((banks[2], banks[3]))])
        nring = bank_ring([6, 7])
        dn_ring = Ring([Slot(banks[4]), Slot(banks[5])] + nring.slots)
        tblocks = blocks(T)
        hb = []
        norm_phase(T, ("ffn_g", i), H, al, ring=nring, blk_toks=hb)

        def gate_up(gi):
            ab = ACTB[gi % 2]
            first = True
            for fi in range(G):
                f = gi * G + fi
                wg = load_wtile(wgate_d[i, f])
                wu = load_wtile(wup_d[i, f])
                for bi, (s, n) in enumerate(tblocks):
                    pp = gu_ring.next()
                    bg, bu = pp.ap
                    for (bk, w) in ((bg, wg), (bu, wu)):
                        for k in range(NCH):
                            t = mm(bk[:, 0:n], w.ap[:, k * 128:(k + 1) * 128], H[:, k, s:s + n], k == 0, k == NCH - 1,
                                   waits=(pp.wwaits() + w.rwaits() + [hb[bi]]) if k == 0 else (), sig=(k == NCH - 1))
                    pp.set_w(t)
                    sg = SG.next()
                    t2 = act.do(lambda e, o=sg.ap[:, 0:n], a=bg[:, 0:n]: e.activation(out=o, in_=a, func=AF.Silu),
                                waits=pp.rwaits() + sg.wwaits(), sig=True)
                    t3 = dve.do(lambda e, o=ab.ap[:, fi, s:s + n], a=sg.ap[:, 0:n], b=bu[:, 0:n]:
                                e.tensor_tensor(out=o, in0=a, in1=b, op=ALU.mult),
                                waits=[t2] + (ab.wwaits() if first else []), sig=True)
                    first = False
                    sg.set_w(t3)
                    pp.add_r(t3)
                wg.add_r(t)
                wu.add_r(t)
            ab.set_w(t3)

        def down(gi):
            ab = ACTB[gi % 2]
            for dc in range(NCH):
                wd = WD.next()
                dma_in(pool, wd, wd.ap.rearrange("p a b -> p (a b)"), wdown_d[i, gi, dc])
                for (s, n) in tblocks:
                    ps = dn_ring.next()
                    for fi in range(G):
                        t = mm(ps.ap[:, 0:n], wd.ap[:, fi, :], ab.ap[:, fi, s:s + n], fi == 0, fi == G - 1,
                               waits=(ps.wwaits() + wd.rwaits() + ab.rwaits()) if fi == 0 else (), sig=(fi == G - 1))
                    ps.set_w(t)
                    t2 = dve.do(lambda e, o=X[:, dc, s:s + n], a=ps.ap[:, 0:n]:
                                e.tensor_tensor(out=o, in0=a, in1=o, op=ALU.add), waits=ps.rwaits(), sig=True)
                    ps.add_r(t2)
                wd.add_r(t)
            ab.add_r(t)

        gate_up(0)
        for gi in range(NG):
            if gi + 1 < NG:
                gate_up(gi + 1)
            down(gi)

    def ple_layer(i):
        T = T_OUT[i]
        H = arena.view(XSZ, (NCH, T), BF16)
        base = XSZ + NCH * T * 2
        P.phase_barrier()
        al = Alloc(arena, base)
        hb = []
        PT = Slot(al.take((2, T), BF16), sem=P.new_sem("pt"))
        WPP = Slot(al.take((2, D), BF16), sem=P.new_sem("wpp"))
        SG = Ring([Slot(al.take((512,), F32)) for _ in range(2)])
        for c in range(2):
            tk = pool.do(lambda e, c=c: e.dma_start(out=PT.ap[:, c, :], in_=pT[i, c, :, 0:T]), inc=(PT.sem, 16))
        PT.set_w(tk)
        dma_in(pool, WPP, WPP.ap.rearrange("p a b -> p (a b)"), wpp_d[i])
        norm_phase(T, ("ple_g", i), H, al, blk_toks=hb)
        ring = Ring([Slot((banks[0], banks[1])), Slot((banks[2], banks[3])), Slot((banks[4], banks[5]))])
        for dc in range(NCH):
            w = load_wtile(wpg_d[i, dc])
            for bi, (s, n) in enumerate(blocks(T)):
                pp = ring.next()
                bg, bp = pp.ap
                for k in range(NCH):
                    t = mm(bg[:, 0:n], w.ap[:, k * 128:(k + 1) * 128], H[:, k, s:s + n], k == 0, k == NCH - 1,
                           waits=(pp.wwaits() + w.rwaits() + [hb[bi]]) if k == 0 else ())
                for c in range(2):
                    t = mm(bp[:, 0:n], WPP.ap[:, c, dc * 128:(dc + 1) * 128], PT.ap[:, c, s:s + n], c == 0, c == 1,
                           waits=(PT.rwaits() + WPP.rwaits()) if c == 0 else (), sig=(c == 1))
                pp.set_w(t)
                sg = SG.next()
                t2 = act.do(lambda e, o=sg.ap[:, 0:n], a=bg[:, 0:n], b=vcol(("ple_b", i), dc):
                            e.activation(out=o, in_=a, func=AF.Sigmoid, bias=b, scale=1.0),
                            waits=pp.rwaits() + sg.wwaits(), sig=True)
                t3 = dve.do(lambda e, o=sg.ap[:, 0:n], b=bp[:, 0:n]: e.tensor_tensor(out=o, in0=o, in1=b, op=ALU.mult),
                            waits=[t2], sig=True)
                pp.add_r(t3)
                t4 = dve.do(lambda e, o=X[:, dc, s:s + n], a=sg.ap[:, 0:n]: e.tensor_tensor(out=o, in0=a, in1=o, op=ALU.add),
                            waits=[t3], sig=True)
                sg.set_w(t4)
            w.add_r(t)

    done = False
    for i in range(DEPTH):
        if i % 2 == 0:
            attention_layer(i)
        else:
            pool_layer(i)
        if stop_after is not None and stop_after[0] == i and stop_after[1] in "mnqa":
            done = True
            break
        ffn_layer(i)
        if stop_after == (i, "f"):
            done = True
            break
        ple_layer(i)
        if stop_after == (i, "p"):
            done = True
            break
    if not done:
        P.phase_barrier()
        al = Alloc(arena, XSZ)
        Hdummy = arena.view(XSZ, (NCH, 1024), BF16)
        norm_phase(1024, "final_g", Hdummy, al, out_f32_X=True)
    P.phase_barrier()
    out_sem = P.new_sem("out")
    for k in range(NCH):
        tk = sp.do(lambda e, k=k: e.dma_start(out=outT[k, :, :], in_=X[:, k, 0:T_DUMP]), inc=(out_sem, 16))
    sp.do(lambda e: e.wait_ge(P.sems[out_sem], P.semcount[out_sem]))
    P.replay()
    return nc, stack


def _local_to_global(half, n):
    l = np.arange(n)
    return l if half == 0 else (SEQ - 1 - l)


def _bias_tables(rpb_j, half):
    outs = []
    for (qrow0, krow0, ncz) in ((0, 0, 4), (2, 0, 4), (8, 4, 5)):
        p = np.arange(128)
        c = np.arange(ncz)
        kl_row = krow0 + 2 * c[:, None] + p[None, :] // 64
        kl_col = np.broadcast_to(p[None, :] % 64, kl_row.shape)
        q = np.arange(128)
        ql_row = qrow0 + q // 64
        ql_col = q % 64
        if half == 0:
            rk, ck, rq, cq = kl_row, kl_col, ql_row, ql_col
        else:
            rk, ck, rq, cq = 31 - kl_row, 63 - kl_col, 31 - ql_row, 63 - ql_col
        rk = rk[:, :, None]
        ck = ck[:, :, None]
        rq = rq[None, None, :]
        cq = cq[None, None, :]
        r_start = np.clip(rq - 4, 0, 24)
        c_start = np.clip(cq - 8, 0, 48)
        valid = (rk >= r_start) & (rk < r_start + 8) & (ck >= c_start) & (ck < c_start + 16)
        dr = np.clip(rk - rq + 7, 0, 14)
        dc = np.clip(ck - cq, -15, 15) + 15
        dr, dc, valid = np.broadcast_arrays(dr, dc, valid)
        gathered = rpb_j[:, dr, dc]
        tab = np.where(valid[None], gathered, np.float32(NEG)).astype(np.float32)
        tab = tab.reshape(16, 4, ncz, 128, 128).transpose(0, 3, 2, 1, 4)
        outs.append(np.ascontiguousarray(tab).reshape(16, 128, 4 * ncz * 128))
    return outs


def _fm(v):
    return np.ascontiguousarray(v.reshape(NCH, 128).T)


def _wtiles(w, ncol_chunks):
    N = w.shape[1]
    t = w.reshape(NCH, 128, N // 128, 128).transpose(2, 1, 0, 3)
    return np.ascontiguousarray(t).reshape(N // 128, 128, NCH * 128)


def prepare_inputs(inp):
    f32 = np.float32
    g = {k: np.asarray(v, dtype=f32) for k, v in inp.items()}
    shared = {}
    shared["wqkv"] = np.stack([_wtiles(g["w_qkv"][j], 48).reshape(3, 16, 128, 2048).transpose(1, 0, 2, 3)
                               for j in range(2)])
    shared["wo"] = np.ascontiguousarray(
        g["w_o"].reshape(2, 8, 2, 128, D).transpose(0, 1, 3, 2, 4))
    shared["wpool"] = np.ascontiguousarray(
        g["w_pool"].reshape(2, 4, 4, 128, 512).transpose(0, 3, 1, 2, 4)).reshape(2, 128, 16 * 512)
    shared["wgate"] = np.stack([_wtiles(g["w_gate"][i], NF) for i in range(DEPTH)])
    shared["wup"] = np.stack([_wtiles(g["w_up"][i], NF) for i in range(DEPTH)])
    shared["wdown"] = np.ascontiguousarray(
        g["w_down"].reshape(DEPTH, 11, 4, 128, NCH, 128).transpose(0, 1, 4, 3, 2, 5)).reshape(DEPTH, 11, 16, 128, 512)
    shared["wpg"] = np.stack([_wtiles(g["w_ple_gate"][i], 16) for i in range(DEPTH)])
    shared["wpp"] = np.ascontiguousarray(
        g["w_ple_proj"].reshape(DEPTH, 2, 128, D).transpose(0, 2, 1, 3)).reshape(DEPTH, 128, 2 * D)
    shared["ident"] = np.eye(128, dtype=f32)
    shared["bv"] = np.ascontiguousarray(np.broadcast_to(g["b_qkv"][:, None, 2 * D:3 * D], (2, 128, D)))
    tabs = {h: [_bias_tables(g["rpb"][j], h) for j in range(2)] for h in (0, 1)}
    vec_common = np.zeros((128, NVEC), f32)
    for i in range(DEPTH):
        j = i // 2
        mg = g["attn_norm_g"][j] if i % 2 == 0 else g["pool_norm_g"][j]
        vec_common[:, VC[("mix_g", i)]:VC[("mix_g", i)] + 16] = _fm(mg)
        vec_common[:, VC[("ffn_g", i)]:VC[("ffn_g", i)] + 16] = _fm(g["ffn_norm_g"][i])
        vec_common[:, VC[("ple_g", i)]:VC[("ple_g", i)] + 16] = _fm(g["ple_norm_g"][i])
        vec_common[:, VC[("ple_b", i)]:VC[("ple_b", i)] + 16] = _fm(g["b_ple_gate"][i])
    for j in range(2):
        vec_common[:, VC[("bq", j)]:VC[("bq", j)] + 16] = _fm(g["b_qkv"][j, 0:D])
        vec_common[:, VC[("bk", j)]:VC[("bk", j)] + 16] = _fm(g["b_qkv"][j, D:2 * D])
        vec_common[:, VC[("pscale", j)]:VC[("pscale", j)] + 16] = _fm(g["pool_scale"][j])
    vec_common[:, VC["final_g"]:VC["final_g"] + 16] = _fm(g["final_norm_g"])
    in_maps = []
    for core in range(8):
        b, half = core // 2, core % 2
        m = dict(shared)
        idx = _local_to_global(half, TX0)
        m["xT"] = np.ascontiguousarray(g["x"][b][idx].T).reshape(NCH, 128, TX0)
        idp = _local_to_global(half, TX)
        m["pT"] = np.ascontiguousarray(g["p"][:, b][:, idp].transpose(0, 2, 1)).reshape(DEPTH, 2, 128, TX)
        v = vec_common.copy()
        v[:, VC["cA"]] = 1.0 if half == 0 else 0.0
        v[:, VC["cB"]] = 0.0 if half == 0 else 1.0
        for gi, w in enumerate(POOL_W):
            t = np.arange(8)
            cnt = np.minimum(w, t + w // 2) if half == 0 else np.minimum(w, t + 1 + w // 2)
            v[:, VC["corr"] + gi * 8: VC["corr"] + gi * 8 + 8] = (w / cnt).astype(f32)[None, :]
        v[:, VC["eps"]] = EPS
        for hh in range(4):
            v[32 * hh:32 * hh + 32, VC["hmask"] + hh] = 1.0
        m["vecs"] = v
        for j in range(2):
            pass
        m["tab0"] = np.stack([tabs[half][j][0] for j in range(2)])
        m["tab1"] = np.stack([tabs[half][j][1] for j in range(2)])
        m["tab2"] = np.stack([tabs[half][j][2] for j in range(2)])
        in_maps.append(m)
    return in_maps


def assemble(results, T):
    out = np.zeros((4, SEQ, D), np.float32)
    for core in range(8):
        b, half = core // 2, core % 2
        o = results[core]["outT"].reshape(D, -1)[:, :T].T
        idx = _local_to_global(half, T)
        out[b, idx] = o
    return out


def kernel(**inputs):
    in_maps = prepare_inputs(inputs)
    nc, stack = build_program(None)
    with stack:
        res = run_bass_kernel_spmd(nc, in_maps, core_ids=list(range(8)))
    return assemble(res.results, 1024)
```

```python
import numpy as np
from contextlib import ExitStack
import concourse.bass as bass
import concourse.mybir as mybir
from concourse.bass_utils import run_bass_kernel_spmd

F32 = mybir.dt.float32
BF16 = mybir.dt.bfloat16
AF = mybir.ActivationFunctionType
ALU = mybir.AluOpType

D = 2048
NCH = 16
DFF = 5632
NF = 44
DEPTH = 4
SEQ = 2048
GW = 64
EPS = 1e-6
NEG = -30000.0
POOL_W = (2, 4, 8, 16)

T_KV = {0: 1664, 2: 1344}
T_OUT = {0: 1352, 1: 1344, 2: 1032, 3: 1024}
T_IN = {0: 1352, 1: 1352, 2: 1344, 3: 1032}
TX = 1352
TX0 = 1664
ARENA_BYTES = 209920

VC = {}
_c = 0
for _i in range(DEPTH):
    for _nm in ("mix_g", "ffn_g", "ple_g", "ple_b"):
        VC[(_nm, _i)] = _c
        _c += 16
for _j in range(2):
    for _nm in ("bq", "bk", "pscale"):
        VC[(_nm, _j)] = _c
        _c += 16
VC["final_g"] = _c
_c += 16
VC["cA"] = _c
VC["cB"] = _c + 1
_c += 2
VC["corr"] = _c
_c += 32
VC["eps"] = _c
_c += 1
VC["hmask"] = _c
_c += 4
NVEC = _c


def blocks(T, maxn=512):
    nb = -(-T // maxn)
    base = -(-T // nb)
    base = -(-base // 8) * 8
    out = []
    s = 0
    while s < T:
        n = min(base, T - s)
        out.append((s, n))
        s += n
    return out


class Slot:
    def __init__(self, ap=None, **kw):
        self.ap = ap
        self.w = []
        self.r = []
        self.__dict__.update(kw)

    def wwaits(self):
        return list(self.r) + list(self.w)

    def set_w(self, *toks):
        self.w = [t for t in toks if t is not None]
        self.r = []

    def add_w(self, tok):
        self.w.append(tok)

    def rwaits(self):
        return list(self.w)

    def add_r(self, tok):
        if tok is not None:
            self.r.append(tok)


class Ring:
    def __init__(self, slots):
        self.slots = slots
        self.i = 0

    def next(self):
        s = self.slots[self.i % len(self.slots)]
        self.i += 1
        return s


class Eng:
    def __init__(self, prog, name, semi):
        self.prog = prog
        self.name = name
        self.semi = semi
        self.ops = []
        self.waited = {}

    def do(self, fn, waits=(), sig=False, inc=None, nobar=False):
        ws = []
        allw = list(waits)
        if not nobar:
            allw += self.prog.barrier
        for t in allw:
            if t is None:
                continue
            s, v = t
            if self.waited.get(s, 0) >= v:
                continue
            self.waited[s] = v
            ws.append((s, v))
        op = {"fn": fn, "ws": ws, "inc": None}
        self.ops.append(op)
        if inc is not None:
            return self._inc(op, inc)
        if sig:
            return self._inc(op, (self.semi, 1))
        return None

    def _inc(self, op, inc):
        assert op["inc"] is None
        op["inc"] = inc
        self.prog.semcount[inc[0]] += inc[1]
        op["tok"] = (inc[0], self.prog.semcount[inc[0]])
        return op["tok"]

    def last_token(self):
        if not self.ops:
            return None
        op = self.ops[-1]
        if op["inc"] is None:
            return self._inc(op, (self.semi, 1))
        return op["tok"]


class Prog:
    def __init__(self, nc, stack):
        self.nc = nc
        self.stack = stack
        self.sems = []
        self.semnames = {}
        self.semcount = []
        self.barrier = []
        self.pe = Eng(self, "pe", self.new_sem("pe"))
        self.act = Eng(self, "act", self.new_sem("act"))
        self.dve = Eng(self, "dve", self.new_sem("dve"))
        self.sp = Eng(self, "sp", None)
        self.pool = Eng(self, "pool", None)

    def new_sem(self, name):
        if name in self.semnames:
            return self.semnames[name]
        self.semnames[name] = len(self.sems)
        s = self.stack.enter_context(self.nc.semaphore(name))
        self.sems.append(s)
        self.semcount.append(0)
        return len(self.sems) - 1

    def phase_barrier(self):
        toks = [e.last_token() for e in (self.pe, self.act, self.dve)]
        self.barrier = [t for t in toks if t is not None]

    def replay(self):
        sems = self.sems

        def run(E, e):
            for op in E.ops:
                for (s, v) in op["ws"]:
                    e.wait_ge(sems[s], v)
                ins = op["fn"](e)
                if op["inc"] is not None:
                    ins.then_inc(sems[op["inc"][0]], op["inc"][1])

        with self.nc.Block() as block:
            @block.tensor
            def _(e):
                run(self.pe, e)

            @block.scalar
            def _(e):
                run(self.act, e)

            @block.vector
            def _(e):
                run(self.dve, e)

            @block.sync
            def _(e):
                run(self.sp, e)

            @block.gpsimd
            def _(e):
                run(self.pool, e)


class Arena:
    def __init__(self, ap, nbytes):
        self.ap = ap
        self.nbytes = nbytes

    def view(self, off, shape, dtype):
        n = int(np.prod(shape))
        assert off % 4 == 0
        if dtype == BF16:
            assert off + 2 * n <= self.nbytes, (off, shape)
            v = self.ap[:, off // 2: off // 2 + n]
        else:
            assert off + 4 * n <= self.nbytes, (off, shape)
            v = self.ap[:, off // 2: off // 2 + 2 * n].bitcast(F32)
        if len(shape) == 2:
            v = v.rearrange("p (a b) -> p a b", b=shape[1])
        elif len(shape) == 3:
            v = v.rearrange("p (a b c) -> p a b c", b=shape[1], c=shape[2])
        return v


class Alloc:
    def __init__(self, arena, base):
        self.arena = arena
        self.off = base

    def take(self, shape, dtype):
        n = int(np.prod(shape)) * (2 if dtype == BF16 else 4)
        n = -(-n // 64) * 64
        v = self.arena.view(self.off, shape, dtype)
        self.off += n
        assert self.off <= self.arena.nbytes - 4 * 4096, ("arena overflow", self.off)
        return v


def build_program(stop_after=None):
    nc = bass.Bass("TRN2", target_bir_lowering=False)
    stack = ExitStack()
    dt = nc.dram_tensor
    xT = dt("xT", [NCH, 128, TX0], F32, kind="ExternalInput").ap()
    pT = dt("pT", [DEPTH, 2, 128, TX], F32, kind="ExternalInput").ap()
    vecs_d = dt("vecs", [128, NVEC], F32, kind="ExternalInput").ap()
    bv_d = dt("bv", [2, 128, D], F32, kind="ExternalInput").ap()
    ident_d = dt("ident", [128, 128], F32, kind="ExternalInput").ap()
    wqkv_d = dt("wqkv", [2, 16, 3, 128, 2048], F32, kind="ExternalInput").ap()
    wo_d = dt("wo", [2, 8, 128, 2, D], F32, kind="ExternalInput").ap()
    tab0_d = dt("tab0", [2, 16, 128, 4 * 4 * 128], F32, kind="ExternalInput").ap()
    tab1_d = dt("tab1", [2, 16, 128, 4 * 4 * 128], F32, kind="ExternalInput").ap()
    tab2_d = dt("tab2", [2, 16, 128, 4 * 5 * 128], F32, kind="ExternalInput").ap()
    wpool_d = dt("wpool", [2, 128, 16 * 512], F32, kind="ExternalInput").ap()
    wgate_d = dt("wgate", [DEPTH, NF, 128, 2048], F32, kind="ExternalInput").ap()
    wup_d = dt("wup", [DEPTH, NF, 128, 2048], F32, kind="ExternalInput").ap()
    wdown_d = dt("wdown", [DEPTH, 11, 16, 128, 4 * 128], F32, kind="ExternalInput").ap()
    wpg_d = dt("wpg", [DEPTH, 16, 128, 2048], F32, kind="ExternalInput").ap()
    wpp_d = dt("wpp", [DEPTH, 128, 2 * D], F32, kind="ExternalInput").ap()
    T_DUMP = 1024 if stop_after is None else TX
    outT = dt("outT", [NCH, 128, T_DUMP], F32, kind="ExternalOutput").ap()

    P = Prog(nc, stack)
    pe, act, dve, sp, pool = P.pe, P.act, P.dve, P.sp, P.pool
    arena_t = stack.enter_context(nc.sbuf_tensor("arena", [128, ARENA_BYTES // 2], BF16))
    arena = Arena(arena_t, ARENA_BYTES)
    vecs = stack.enter_context(nc.sbuf_tensor("vecs_sb", [128, NVEC], F32))
    ones = stack.enter_context(nc.sbuf_tensor("ones_sb", [128, 128], BF16))
    ident = stack.enter_context(nc.sbuf_tensor("ident_sb", [128, 128], BF16))
    bqs = stack.enter_context(nc.sbuf_tensor("bqs_sb", [128, 16], F32))
    sbig = stack.enter_context(nc.psum_tensor("sbig", [128, 2048], F32))
    banks = [sbig[:, i * 512:(i + 1) * 512] for i in range(4)]
    banks += [stack.enter_context(nc.psum_tensor(f"bank{i}", [128, 512], F32))[:, :] for i in range(4, 8)]
    sview = sbig[:, :].rearrange("p (a s b) -> p a s b", s=4, b=128)

    XSZ = NCH * TX * 4
    X = arena.view(0, (NCH, TX), F32)
    NWT = 4
    WT_BASE = ARENA_BYTES - NWT * 4096
    wt_slots = [Slot(arena.view(WT_BASE + i * 4096, (2048,), BF16), sem=P.new_sem(f"wt{i}")) for i in range(NWT)]
    wt_ring = Ring(wt_slots)

    def vcol(key, k=0, n=1):
        c = VC[key] + k
        return vecs[:, c:c + n]

    def dma_in(E, slot, out_ap, in_ap, nobar=False, extra_waits=()):
        tok = E.do(lambda e, o=out_ap, i=in_ap: e.dma_start(out=o, in_=i),
                   waits=slot.wwaits() + list(extra_waits), inc=(slot.sem, 16), nobar=nobar)
        slot.set_w(tok)
        return tok

    def load_wtile(src_ap):
        s = wt_ring.next()
        dma_in(pool, s, s.ap, src_ap, nobar=True)
        return s

    def mm(out, lhsT, rhs, start, stop, waits=(), sig=False, tp=None):
        if tp is None:
            fn = lambda e: e.matmul(out, lhsT=lhsT, rhs=rhs, start=start, stop=stop)
        else:
            fn = lambda e: e.matmul(out, lhsT=lhsT, rhs=rhs, start=start, stop=stop, tile_position=tp)
        return pe.do(fn, waits=waits, sig=sig)

    def bank_ring(idxs):
        return Ring([Slot(banks[i]) for i in idxs])

    vec_slot = Slot(sem=P.new_sem("vecs"))
    dma_in(sp, vec_slot, vecs[:, :], vecs_d[:, :])
    x_slot = Slot(sem=P.new_sem("xload"))
    xtoks = []
    for k in range(NCH):
        xtoks.append(sp.do(lambda e, k=k: e.dma_start(out=X[:, k, :], in_=xT[k, :, 0:TX]),
                           inc=(x_slot.sem, 16)))
    x_ready = [xtoks[-1]]
    ones_tok = dve.do(lambda e: e.memset(ones[:, :], 1.0), sig=True)
    id_slot = Slot(sem=P.new_sem("ident"))
    dma_in(pool, id_slot, ident[:, :], ident_d[:, :])
    const_ready = [ones_tok] + vec_slot.rwaits() + id_slot.rwaits()

    def norm_phase(T, gkey, H, al, src=None, tok_off=0, keep_rstd=False, out_f32_X=False, extra_waits=(), ring=None,
                   blk_toks=None):
        if src is None:
            src = lambda k, s, n: X[:, k, s:s + n]
        SQ = Ring([Slot(al.take((512,), BF16)) for _ in range(3)])
        TMP = Slot(al.take((512,), F32))
        RSTD = al.take((T,), F32)
        if ring is None:
            ring = bank_ring([6, 7])
        ew = list(extra_waits) + const_ready + x_ready
        for (s, n) in blocks(T):
            ps = ring.next()
            for k in range(NCH):
                sq = SQ.next()
                t = act.do(lambda e, o=sq.ap[:, 0:n], i=src(k, s, n): e.activation(out=o, in_=i, func=AF.Square),
                           waits=sq.wwaits() + ew, sig=True)
                sq.set_w(t)
                t = mm(ps.ap[:, 0:n], ones[:, :], sq.ap[:, 0:n], k == 0, k == NCH - 1,
                       waits=sq.rwaits() + (ps.wwaits() if k == 0 else []) + ew, sig=True)
                sq.add_r(t)
            ps.set_w(t)
            t = act.do(lambda e, o=TMP.ap[:, 0:n], i=ps.ap[:, 0:n]: e.activation(
                out=o, in_=i, func=AF.Sqrt, bias=vcol("eps"), scale=1.0 / D),
                waits=ps.rwaits() + TMP.wwaits(), sig=True)
            ps.add_r(t)
            TMP.set_w(t)
            t = dve.do(lambda e, o=RSTD[:, s:s + n], i=TMP.ap[:, 0:n]: e.reciprocal(out=o, in_=i),
                       waits=TMP.rwaits(), sig=True)
            TMP.add_r(t)
            if H is not None:
                for k in range(NCH):
                    o = X[:, k, s:s + n] if out_f32_X else H[:, k, tok_off + s: tok_off + s + n]
                    t2 = dve.do(lambda e, o=o, i=src(k, s, n), g=vcol(gkey, k), r=RSTD[:, s:s + n]:
                                e.scalar_tensor_tensor(out=o, in0=i, scalar=g, in1=r, op0=ALU.mult, op1=ALU.mult),
                                waits=[t] + ew, sig=(k == NCH - 1))
                if blk_toks is not None:
                    blk_toks.append(t2)
        return RSTD, t

    def attention_layer(i):
        j = i // 2
        Tkv, Tq = T_KV[i], T_OUT[i]
        H = arena.view(XSZ, (NCH, Tkv), BF16)
        base = XSZ + NCH * Tkv * 2
        P.phase_barrier()
        al = Alloc(arena, base)
        if i == 0:
            norm_phase(TX, ("mix_g", i), H, al)
            nh = Tkv - TX
            XH = al.take((NCH, nh), F32)
            xh_slot = Slot(sem=P.new_sem("xh"))
            for k in range(NCH):
                tk = sp.do(lambda e, k=k: e.dma_start(out=XH[:, k, :], in_=xT[k, :, TX:Tkv]),
                           inc=(xh_slot.sem, 16))
            al2 = Alloc(arena, al.off)
            norm_phase(nh, ("mix_g", i), H, al2, src=lambda k, s, n: XH[:, k, s:s + n], tok_off=TX,
                       extra_waits=[tk])
        else:
            norm_phase(Tkv, ("mix_g", i), H, al)
        if stop_after == (i, "n"):
            return
        P.phase_barrier()
        al = Alloc(arena, base)
        nkc = Tkv // 128 if Tkv % 128 == 0 else Tkv // 128 + 1
        scale = 32 ** -0.5
        QKV = []
        for b in range(2):
            QKV.append(dict(Q=Slot(al.take((Tq,), BF16)), K=Slot(al.take((Tkv,), BF16)),
                            V=Slot(al.take((nkc, 128), BF16))))
        TB = [Slot(al.take((4 * 512,), BF16), sem=P.new_sem("tb0")),
              Slot(al.take((4 * 512,), BF16), sem=P.new_sem("tb1")),
              Slot(al.take((5 * 512,), BF16), sem=P.new_sem("tb2"))]
        tbq = dve.do(lambda e: e.tensor_scalar(out=bqs[:, :], in0=vcol(("bq", j), 0, 16), scalar1=scale, scalar2=None,
                                               op0=ALU.mult), waits=const_ready, sig=True)
        tabs_d = [tab0_d, tab1_d, tab2_d]
        OT = [Slot(al.take((2, Tq), BF16))]
        WO = Slot(al.take((2, D), BF16), sem=P.new_sem("wo"))
        PR = Ring([Slot(al.take((4, 128), BF16)) for _ in range(3)])
        RC = Ring([Slot(al.take((128,), F32)) for _ in range(2)])
        BVB = Ring([Slot(al.take((128,), F32), sem=P.new_sem(f"bvb{q}")) for q in range(2)])
        s_ring = bank_ring([0, 1])
        o_ring = Ring([Slot((banks[2], banks[3])), Slot((banks[4], banks[5]))])
        g_ring = bank_ring([6, 7])
        wo_ring = Ring(s_ring.slots + g_ring.slots)
        QB = Ring([Slot(al.take((4, 128), BF16)) for _ in range(2)])
        for qbs in QB.slots:
            qbs.set_w(dve.do(lambda e, o=qbs.ap: e.memset(o, 0.0), sig=True))
        qblocks = blocks(Tq)
        kblocks = blocks(Tkv)
        wtiles = {}
        bvbs = {}

        def dma_w(g):
            wtiles[g] = [load_wtile(wqkv_d[j, g, s]) for s in range(3)]
            sl = BVB.next()
            dma_in(sp, sl, sl.ap, bv_d[j, :, g * 128:(g + 1) * 128])
            bvbs[g] = sl

        def dma_tab(g):
            for t in range(3):
                sl = TB[t]
                dma_in(pool, sl, sl.ap, tabs_d[t][j, g])

        def qkv(g):
            buf = QKV[g % 2]
            wq, wk, wv = wtiles[g]
            for (dst, w, bkey, blks) in ((buf["Q"], wq, None, qblocks), (buf["K"], wk, "bk", kblocks)):
                first = True
                toks = []
                for (s, n) in blks:
                    ps = g_ring.next()
                    for k in range(NCH):
                        t = mm(ps.ap[:, 0:n], w.ap[:, k * 128:(k + 1) * 128], H[:, k, s:s + n], k == 0, k == NCH - 1,
                               waits=(ps.wwaits() + w.rwaits()) if k == 0 else (), sig=(k == NCH - 1))
                    ps.set_w(t)
                    bb = bqs[:, g:g + 1] if bkey is None else vcol((bkey, j), g)
                    sc = scale if bkey is None else 1.0
                    t2 = act.do(lambda e, o=dst.ap[:, s:s + n], i=ps.ap[:, 0:n], b=bb, sc=sc:
                                e.activation(out=o, in_=i, func=AF.Identity, bias=b, scale=sc),
                                waits=ps.rwaits() + (dst.wwaits() if first else []) + [tbq], sig=True)
                    first = False
                    ps.add_r(t2)
                    toks.append(t2)
                w.add_r(t)
                dst.set_w(toks[-1])
            dstv = buf["V"]
            bvb = bvbs[g]
            first = True
            for c0 in range(0, nkc, 4):
                ps = g_ring.next()
                ncs = min(4, nkc - c0)
                for cc in range(ncs):
                    c = c0 + cc
                    nk = min(128, Tkv - c * 128)
                    for k in range(NCH):
                        t = mm(ps.ap[0:nk, cc * 128:(cc + 1) * 128], H[:, k, c * 128:c * 128 + nk],
                               wv.ap[:, k * 128:(k + 1) * 128], k == 0, k == NCH - 1,
                               waits=(ps.wwaits() + wv.rwaits()) if (k == 0 and cc == 0) else (),
                               sig=(k == NCH - 1 and cc == ncs - 1))
                ps.set_w(t)
                for cc in range(ncs):
                    c = c0 + cc
                    nk = min(128, Tkv - c * 128)
                    t2 = dve.do(lambda e, o=dstv.ap[0:nk, c, :], a=ps.ap[0:nk, cc * 128:(cc + 1) * 128], b=bvb.ap[0:nk, :]:
                                e.tensor_tensor(out=o, in0=a, in1=b, op=ALU.add),
                                waits=ps.rwaits() + bvb.rwaits() + (dstv.wwaits() if first else []), sig=(cc == ncs - 1))
                    first = False
                ps.add_r(t2)
            wv.add_r(t)
            bvb.add_r(t2)
            dstv.set_w(t2)

        def attend(g):
            buf = QKV[g % 2]
            Qs, Ks, Vs = buf["Q"], buf["K"], buf["V"]
            ot = OT[0]
            gi = g % 2
            npair = -(-Tq // 128)
            last_pv = None
            q_last_tok = [None]
            for pj in range(npair):
                q0 = pj * 128
                nq = min(128, Tq - q0)
                if pj < 2:
                    tb, cs, ncz = TB[pj], 0, 4
                else:
                    tb, cs, ncz = TB[2], pj - 2, 5
                ob = o_ring.next()
                chunks = []
                for c in range(ncz):
                    kc = cs + c
                    nk = min(128, Tkv - kc * 128)
                    assert nk in (64, 128)
                    chunks.append((c, kc, nk))
                qb = QB.next()
                qb_last = [None]
                for hh in range(4):
                    tq = dve.do(lambda e, o=qb.ap[:, hh, 0:nq], i=Qs.ap[:, q0:q0 + nq], m=vcol("hmask", hh):
                                e.tensor_scalar(out=o, in0=i, scalar1=m, scalar2=None, op0=ALU.mult),
                                waits=(qb.wwaits() + Qs.rwaits()) if hh == 0 else (), sig=(hh == 3))
                qb.set_w(tq)
                q_last_tok[0] = tq

                def issue_s(c, kc, nk):
                    sb = s_ring.next()
                    sbv = sb.ap.rearrange("p (a b) -> p a b", b=128)[0:nk, :, 0:nq]
                    t = mm(sb.ap[0:nk, :], Ks.ap[:, kc * 128:kc * 128 + nk], qb.ap.rearrange("p a b -> p (a b)"), True, False,
                           waits=sb.wwaits() + qb.rwaits() + Ks.rwaits())
                    t = mm(sb.ap[0:nk, :], ident[0:nk, 0:nk], tb.ap[0:nk, c * 512:(c + 1) * 512], False, True,
                           waits=tb.rwaits(), sig=True)
                    qb_last[0] = t
                    sb.set_w(t)
                    pr = PR.next()
                    t2 = act.do(lambda e, o=pr.ap[0:nk, :, 0:nq], i=sbv: e.activation(out=o, in_=i, func=AF.Exp),
                                waits=sb.rwaits() + pr.wwaits(), sig=True)
                    sb.add_r(t2)
                    pr.set_w(t2)
                    return pr

                def issue_pv(c, kc, nk, pr):
                    nonlocal last_pv
                    for hh in range(4):
                        w = (pr.rwaits() + Vs.rwaits() + (ob.wwaits() if c == 0 else [])) if hh == 0 else ()
                        mm(ob.ap[0][32 * hh:32 * hh + 32, 0:nq], Vs.ap[0:nk, kc, 32 * hh:32 * hh + 32], pr.ap[0:nk, hh, 0:nq],
                           c == 0, c == ncz - 1, waits=w, tp=(0, 32 * hh))
                    for hh in range(4):
                        t = mm(ob.ap[1][32 * hh:32 * hh + 32, 0:nq], ones[0:nk, 0:32], pr.ap[0:nk, hh, 0:nq],
                               c == 0, c == ncz - 1, sig=(hh == 3), tp=(0, 32 * hh))
                    pr.add_r(t)
                    last_pv = t
                    return t

                prs = {}
                prs[0] = issue_s(*chunks[0])
                for c in range(ncz):
                    if c + 1 < ncz:
                        prs[c + 1] = issue_s(*chunks[c + 1])
                    t = issue_pv(*chunks[c], prs[c])
                ob.set_w(t)
                qb.add_r(qb_last[0])
                rc = RC.next()
                t1 = dve.do(lambda e, o=rc.ap[:, 0:nq], i=ob.ap[1][:, 0:nq]: e.reciprocal(out=o, in_=i),
                            waits=ob.rwaits() + rc.wwaits(), sig=True)
                t2 = dve.do(lambda e, o=ot.ap[:, gi, q0:q0 + nq], a=ob.ap[0][:, 0:nq], b=rc.ap[:, 0:nq]:
                            e.tensor_tensor(out=o, in0=a, in1=b, op=ALU.mult),
                            waits=[t1] + (ot.wwaits() if (pj == 0 and gi == 0) else []), sig=True)
                rc.set_w(t2)
                ob.add_r(t2)
            Qs.add_r(q_last_tok[0])
            Ks.add_r(last_pv)
            Vs.add_r(last_pv)
            if gi == 0:
                ot.set_w(t2)
            else:
                ot.add_w(t2)

        def wo_dma(s):
            flat = WO.ap.rearrange("p a b -> p (a b)")
            dma_in(pool, WO, flat, wo_d[j, s].rearrange("p a b -> p (a b)"))

        def wo_set(s):
            ot = OT[0]
            for dc in range(NCH):
                for (s0, n) in qblocks:
                    ps = wo_ring.next()
                    for gi in range(2):
                        t = mm(ps.ap[:, 0:n], WO.ap[:, gi, dc * 128:(dc + 1) * 128], ot.ap[:, gi, s0:s0 + n], gi == 0, gi == 1,
                               waits=(ps.wwaits() + WO.rwaits() + ot.rwaits()) if gi == 0 else (), sig=(gi == 1))
                    ps.set_w(t)
                    t2 = dve.do(lambda e, o=X[:, dc, s0:s0 + n], a=ps.ap[:, 0:n]:
                                e.tensor_tensor(out=o, in0=a, in1=o, op=ALU.add),
                                waits=ps.rwaits() + x_ready, sig=True)
                    ps.add_r(t2)
            WO.add_r(t)
            ot.add_r(t)

        def attend_body(g):
            attend(g)
            tok = pe.last_token()
            for t in range(3):
                TB[t].add_r(tok)

        dma_w(0)
        dma_tab(0)
        qkv(0)
        if stop_after == (i, "q"):
            return
        if stop_after == (i, "a"):
            attend_body(0)
            return
        if 1 < 16:
            dma_w(1)
        for g in range(16):
            if g + 1 < 16:
                qkv(g + 1)
            if g + 2 < 16:
                dma_w(g + 2)
            if g % 2 == 1:
                wo_dma(g // 2)
            attend_body(g)
            if g + 1 < 16:
                dma_tab(g + 1)
            if g % 2 == 1:
                wo_set(g // 2)

    def pool_layer(i):
        j = i // 2
        Tin, Tout = T_IN[i], T_OUT[i]
        H = arena.view(XSZ, (NCH, Tin), BF16)
        base = XSZ + NCH * Tin * 2
        P.phase_barrier()
        al = Alloc(arena, base)
        WP = Slot(al.take((16, 512), BF16), sem=P.new_sem("wp"))
        dma_in(pool, WP, WP.ap.rearrange("p a b -> p (a b)"), wpool_d[j])
        RSTD, rtok = norm_phase(Tin, None, None, al)
        PADL = 8
        HF = [Slot(al.take((PADL + Tin,), F32)) for _ in range(2)]
        PA = al.take((PADL + Tin,), F32)
        PB = al.take((PADL + Tin,), F32)
        for hf in HF:
            t = dve.do(lambda e, o=hf.ap[:, 0:PADL]: e.memset(o, 0.0), sig=True)
            hf.set_w(t)
        ring = bank_ring([0, 1, 2, 3])
        oblocks = blocks(Tout)
        prev_tl = rtok
        for k in range(NCH):
            grp = k // 4
            w = POOL_W[grp]
            hf = HF[k % 2]
            h = hf.ap
            tl = dve.do(lambda e, o=h[:, PADL:PADL + Tin], x=X[:, k, 0:Tin], g=vcol(("mix_g", i), k), r=RSTD[:, 0:Tin]:
                        e.scalar_tensor_tensor(out=o, in0=x, scalar=g, in1=r, op0=ALU.mult, op1=ALU.mult),
                        waits=hf.wwaits() + const_ready + [prev_tl], sig=True)
            L = PADL + Tin
            cur = h
            curw = 1
            bufs = [PA, PB]
            bi = 0
            while curw < w:
                nxt = bufs[bi]
                bi ^= 1
                n = L - 2 * curw + 1
                tl = dve.do(lambda e, o=nxt[:, 0:n], a=cur[:, 0:n], b=cur[:, curw:curw + n]:
                            e.tensor_tensor(out=o, in0=a, in1=b, op=ALU.add), waits=[tl], sig=True)
                cur = nxt
                curw *= 2
            U = bufs[bi]
            sA = PADL - w // 2
            tl = dve.do(lambda e, o=U[:, 0:Tout], a=cur[:, sA:sA + Tout], c=vcol("cA"):
                        e.tensor_scalar(out=o, in0=a, scalar1=c, scalar2=None, op0=ALU.mult), waits=[tl], sig=True)
            tl = dve.do(lambda e, o=U[:, 0:Tout], a=cur[:, sA + 1:sA + 1 + Tout], c=vcol("cB"):
                        e.scalar_tensor_tensor(out=o, in0=a, scalar=c, in1=o, op0=ALU.mult, op1=ALU.add),
                        waits=[tl], sig=True)
            tl = dve.do(lambda e, o=U[:, 0:8], c=vcol("corr", grp * 8, 8):
                        e.tensor_tensor(out=o, in0=o, in1=c, op=ALU.mult), waits=[tl], sig=True)
            tl = dve.do(lambda e, o=H[:, k, 0:Tout], a=U[:, 0:Tout], b=h[:, PADL:PADL + Tout], iw=1.0 / w:
                        e.scalar_tensor_tensor(out=o, in0=a, scalar=iw, in1=b, op0=ALU.mult, op1=ALU.subtract),
                        waits=[tl], sig=True)
            hf.set_w(tl)
            prev_tl = tl
            if k % 4 == 3:
                for ec in range(4):
                    for (s0, n) in oblocks:
                        ps = ring.next()
                        for c in range(4):
                            t = mm(ps.ap[:, 0:n], WP.ap[:, grp * 4 + c, ec * 128:(ec + 1) * 128], H[:, grp * 4 + c, s0:s0 + n],
                                   c == 0, c == 3, waits=(ps.wwaits() + WP.rwaits() + [tl]) if c == 0 else (), sig=(c == 3))
                        ps.set_w(t)
                        ko = grp * 4 + ec
                        t2 = dve.do(lambda e, o=X[:, ko, s0:s0 + n], a=ps.ap[:, 0:n], sc=vcol(("pscale", j), ko):
                                    e.scalar_tensor_tensor(out=o, in0=a, scalar=sc, in1=o, op0=ALU.mult, op1=ALU.add),
                                    waits=ps.rwaits(), sig=True)
                        ps.add_r(t2)

    def ffn_layer(i):
        T = T_OUT[i]
        H = arena.view(XSZ, (NCH, T), BF16)
        base = XSZ + NCH * T * 2
        P.phase_barrier()
        al = Alloc(arena, base)
        G = 4
        NG = NF // G
        ACTB = [Slot(al.take((G, T), BF16)) for _ in range(2)]
        SG = Ring([Slot(al.take((512,), F32)) for _ in range(2)])
        WD = Ring([Slot(al.take((G, 128), BF16), sem=P.new_sem(f"wd{q}")) for q in range(6)])
        gu_ring = Ring([Slot((banks[0], banks[1])), Slot((banks[2], banks[3]))])
        nring = bank_ring([6, 7])
        dn_ring = Ring([Slot(banks[4]), Slot(banks[5])] + nring.slots)
        tblocks = blocks(T)
        hb = []
        norm_phase(T, ("ffn_g", i), H, al, ring=nring, blk_toks=hb)

        def gate_up(gi):
            ab = ACTB[gi % 2]
            first = True
            for fi in range(G):
                f = gi * G + fi
                wg = load_wtile(wgate_d[i, f])
                wu = load_wtile(wup_d[i, f])
                for bi, (s, n) in enumerate(tblocks):
                    pp = gu_ring.next()
                    bg, bu = pp.ap
                    for (bk, w) in ((bg, wg), (bu, wu)):
                        for k in range(NCH):
                            t = mm(bk[:, 0:n], w.ap[:, k * 128:(k + 1) * 128], H[:, k, s:s + n], k == 0, k == NCH - 1,
                                   waits=(pp.wwaits() + w.rwaits() + [hb[bi]]) if k == 0 else (), sig=(k == NCH - 1))
                    pp.set_w(t)
                    sg = SG.next()
                    t2 = act.do(lambda e, o=sg.ap[:, 0:n], a=bg[:, 0:n]: e.activation(out=o, in_=a, func=AF.Silu),
                                waits=pp.rwaits() + sg.wwaits(), sig=True)
                    t3 = dve.do(lambda e, o=ab.ap[:, fi, s:s + n], a=sg.ap[:, 0:n], b=bu[:, 0:n]:
                                e.tensor_tensor(out=o, in0=a, in1=b, op=ALU.mult),
                                waits=[t2] + (ab.wwaits() if first else []), sig=True)
                    first = False
                    sg.set_w(t3)
                    pp.add_r(t3)
                wg.add_r(t)
                wu.add_r(t)
            ab.set_w(t3)

        def down(gi):
            ab = ACTB[gi % 2]
            for dc in range(NCH):
                wd = WD.next()
                dma_in(pool, wd, wd.ap.rearrange("p a b -> p (a b)"), wdown_d[i, gi, dc])
                for (s, n) in tblocks:
                    ps = dn_ring.next()
                    for fi in range(G):
                        t = mm(ps.ap[:, 0:n], wd.ap[:, fi, :], ab.ap[:, fi, s:s + n], fi == 0, fi == G - 1,
                               waits=(ps.wwaits() + wd.rwaits() + ab.rwaits()) if fi == 0 else (), sig=(fi == G - 1))
                    ps.set_w(t)
                    t2 = dve.do(lambda e, o=X[:, dc, s:s + n], a=ps.ap[:, 0:n]:
                                e.tensor_tensor(out=o, in0=a, in1=o, op=ALU.add), waits=ps.rwaits(), sig=True)
                    ps.add_r(t2)
                wd.add_r(t)
            ab.add_r(t)

        gate_up(0)
        for gi in range(NG):
            if gi + 1 < NG:
                gate_up(gi + 1)
            down(gi)

    def ple_layer(i):
        T = T_OUT[i]
        H = arena.view(XSZ, (NCH, T), BF16)
        base = XSZ + NCH * T * 2
        P.phase_barrier()
        al = Alloc(arena, base)
        hb = []
        PT = Slot(al.take((2, T), BF16), sem=P.new_sem("pt"))
        WPP = Slot(al.take((2, D), BF16), sem=P.new_sem("wpp"))
        SG = Ring([Slot(al.take((512,), F32)) for _ in range(2)])
        for c in range(2):
            tk = pool.do(lambda e, c=c: e.dma_start(out=PT.ap[:, c, :], in_=pT[i, c, :, 0:T]), inc=(PT.sem, 16))
        PT.set_w(tk)
        dma_in(pool, WPP, WPP.ap.rearrange("p a b -> p (a b)"), wpp_d[i])
        norm_phase(T, ("ple_g", i), H, al, blk_toks=hb)
        ring = Ring([Slot((banks[0], banks[1])), Slot((banks[2], banks[3])), Slot((banks[4], banks[5]))])
        for dc in range(NCH):
            w = load_wtile(wpg_d[i, dc])
            for bi, (s, n) in enumerate(blocks(T)):
                pp = ring.next()
                bg, bp = pp.ap
                for k in range(NCH):
                    t = mm(bg[:, 0:n], w.ap[:, k * 128:(k + 1) * 128], H[:, k, s:s + n], k == 0, k == NCH - 1,
                           waits=(pp.wwaits() + w.rwaits() + [hb[bi]]) if k == 0 else ())
                for c in range(2):
                    t = mm(bp[:, 0:n], WPP.ap[:, c, dc * 128:(dc + 1) * 128], PT.ap[:, c, s:s + n], c == 0, c == 1,
                           waits=(PT.rwaits() + WPP.rwaits()) if c == 0 else (), sig=(c == 1))
                pp.set_w(t)
                sg = SG.next()
                t2 = act.do(lambda e, o=sg.ap[:, 0:n], a=bg[:, 0:n], b=vcol(("ple_b", i), dc):
                            e.activation(out=o, in_=a, func=AF.Sigmoid, bias=b, scale=1.0),
                            waits=pp.rwaits() + sg.wwaits(), sig=True)
                t3 = dve.do(lambda e, o=sg.ap[:, 0:n], b=bp[:, 0:n]: e.tensor_tensor(out=o, in0=o, in1=b, op=ALU.mult),
                            waits=[t2], sig=True)
                pp.add_r(t3)
                t4 = dve.do(lambda e, o=X[:, dc, s:s + n], a=sg.ap[:, 0:n]: e.tensor_tensor(out=o, in0=a, in1=o, op=ALU.add),
                            waits=[t3], sig=True)
                sg.set_w(t4)
            w.add_r(t)

    done = False
    for i in range(DEPTH):
        if i % 2 == 0:
            attention_layer(i)
        else:
            pool_layer(i)
        if stop_after is not None and stop_after[0] == i and stop_after[1] in "mnqa":
            done = True
            break
        ffn_layer(i)
        if stop_after == (i, "f"):
            done = True
            break
        ple_layer(i)
        if stop_after == (i, "p"):
            done = True
            break
    if not done:
        P.phase_barrier()
        al = Alloc(arena, XSZ)
        Hdummy = arena.view(XSZ, (NCH, 1024), BF16)
        norm_phase(1024, "final_g", Hdummy, al, out_f32_X=True)
    P.phase_barrier()
    out_sem = P.new_sem("out")
    for k in range(NCH):
        tk = sp.do(lambda e, k=k: e.dma_start(out=outT[k, :, :], in_=X[:, k, 0:T_DUMP]), inc=(out_sem, 16))
    sp.do(lambda e: e.wait_ge(P.sems[out_sem], P.semcount[out_sem]))
    P.replay()
    return nc, stack


def _local_to_global(half, n):
    l = np.arange(n)
    return l if half == 0 else (SEQ - 1 - l)


def _bias_tables(rpb_j, half):
    outs = []
    for (qrow0, krow0, ncz) in ((0, 0, 4), (2, 0, 4), (8, 4, 5)):
        p = np.arange(128)
        c = np.arange(ncz)
        kl_row = krow0 + 2 * c[:, None] + p[None, :] // 64
        kl_col = np.broadcast_to(p[None, :] % 64, kl_row.shape)
        q = np.arange(128)
        ql_row = qrow0 + q // 64
        ql_col = q % 64
        if half == 0:
            rk, ck, rq, cq = kl_row, kl_col, ql_row, ql_col
        else:
            rk, ck, rq, cq = 31 - kl_row, 63 - kl_col, 31 - ql_row, 63 - ql_col
        rk = rk[:, :, None]
        ck = ck[:, :, None]
        rq = rq[None, None, :]
        cq = cq[None, None, :]
        r_start = np.clip(rq - 4, 0, 24)
        c_start = np.clip(cq - 8, 0, 48)
        valid = (rk >= r_start) & (rk < r_start + 8) & (ck >= c_start) & (ck < c_start + 16)
        dr = np.clip(rk - rq + 7, 0, 14)
        dc = np.clip(ck - cq, -15, 15) + 15
        dr, dc, valid = np.broadcast_arrays(dr, dc, valid)
        gathered = rpb_j[:, dr, dc]
        tab = np.where(valid[None], gathered, np.float32(NEG)).astype(np.float32)
        tab = tab.reshape(16, 4, ncz, 128, 128).transpose(0, 3, 2, 1, 4)
        outs.append(np.ascontiguousarray(tab).reshape(16, 128, 4 * ncz * 128))
    return outs


def _fm(v):
    return np.ascontiguousarray(v.reshape(NCH, 128).T)


def _wtiles(w, ncol_chunks):
    N = w.shape[1]
    t = w.reshape(NCH, 128, N // 128, 128).transpose(2, 1, 0, 3)
    return np.ascontiguousarray(t).reshape(N // 128, 128, NCH * 128)


def prepare_inputs(inp):
    f32 = np.float32
    g = {k: np.asarray(v, dtype=f32) for k, v in inp.items()}
    shared = {}
    shared["wqkv"] = np.stack([_wtiles(g["w_qkv"][j], 48).reshape(3, 16, 128, 2048).transpose(1, 0, 2, 3)
                               for j in range(2)])
    shared["wo"] = np.ascontiguousarray(
        g["w_o"].reshape(2, 8, 2, 128, D).transpose(0, 1, 3, 2, 4))
    shared["wpool"] = np.ascontiguousarray(
        g["w_pool"].reshape(2, 4, 4, 128, 512).transpose(0, 3, 1, 2, 4)).reshape(2, 128, 16 * 512)
    shared["wgate"] = np.stack([_wtiles(g["w_gate"][i], NF) for i in range(DEPTH)])
    shared["wup"] = np.stack([_wtiles(g["w_up"][i], NF) for i in range(DEPTH)])
    shared["wdown"] = np.ascontiguousarray(
        g["w_down"].reshape(DEPTH, 11, 4, 128, NCH, 128).transpose(0, 1, 4, 3, 2, 5)).reshape(DEPTH, 11, 16, 128, 512)
    shared["wpg"] = np.stack([_wtiles(g["w_ple_gate"][i], 16) for i in range(DEPTH)])
    shared["wpp"] = np.ascontiguousarray(
        g["w_ple_proj"].reshape(DEPTH, 2, 128, D).transpose(0, 2, 1, 3)).reshape(DEPTH, 128, 2 * D)
    shared["ident"] = np.eye(128, dtype=f32)
    shared["bv"] = np.ascontiguousarray(np.broadcast_to(g["b_qkv"][:, None, 2 * D:3 * D], (2, 128, D)))
    tabs = {h: [_bias_tables(g["rpb"][j], h) for j in range(2)] for h in (0, 1)}
    vec_common = np.zeros((128, NVEC), f32)
    for i in range(DEPTH):
        j = i // 2
        mg = g["attn_norm_g"][j] if i % 2 == 0 else g["pool_norm_g"][j]
        vec_common[:, VC[("mix_g", i)]:VC[("mix_g", i)] + 16] = _fm(mg)
        vec_common[:, VC[("ffn_g", i)]:VC[("ffn_g", i)] + 16] = _fm(g["ffn_norm_g"][i])
        vec_common[:, VC[("ple_g", i)]:VC[("ple_g", i)] + 16] = _fm(g["ple_norm_g"][i])
        vec_common[:, VC[("ple_b", i)]:VC[("ple_b", i)] + 16] = _fm(g["b_ple_gate"][i])
    for j in range(2):
        vec_common[:, VC[("bq", j)]:VC[("bq", j)] + 16] = _fm(g["b_qkv"][j, 0:D])
        vec_common[:, VC[("bk", j)]:VC[("bk", j)] + 16] = _fm(g["b_qkv"][j, D:2 * D])
        vec_common[:, VC[("pscale", j)]:VC[("pscale", j)] + 16] = _fm(g["pool_scale"][j])
    vec_common[:, VC["final_g"]:VC["final_g"] + 16] = _fm(g["final_norm_g"])
    in_maps = []
    for core in range(8):
        b, half = core // 2, core % 2
        m = dict(shared)
        idx = _local_to_global(half, TX0)
        m["xT"] = np.ascontiguousarray(g["x"][b][idx].T).reshape(NCH, 128, TX0)
        idp = _local_to_global(half, TX)
        m["pT"] = np.ascontiguousarray(g["p"][:, b][:, idp].transpose(0, 2, 1)).reshape(DEPTH, 2, 128, TX)
        v = vec_common.copy()
        v[:, VC["cA"]] = 1.0 if half == 0 else 0.0
        v[:, VC["cB"]] = 0.0 if half == 0 else 1.0
        for gi, w in enumerate(POOL_W):
            t = np.arange(8)
            cnt = np.minimum(w, t + w // 2) if half == 0 else np.minimum(w, t + 1 + w // 2)
            v[:, VC["corr"] + gi * 8: VC["corr"] + gi * 8 + 8] = (w / cnt).astype(f32)[None, :]
        v[:, VC["eps"]] = EPS
        for hh in range(4):
            v[32 * hh:32 * hh + 32, VC["hmask"] + hh] = 1.0
        m["vecs"] = v
        for j in range(2):
            pass
        m["tab0"] = np.stack([tabs[half][j][0] for j in range(2)])
        m["tab1"] = np.stack([tabs[half][j][1] for j in range(2)])
        m["tab2"] = np.stack([tabs[half][j][2] for j in range(2)])
        in_maps.append(m)
    return in_maps


def assemble(results, T):
    out = np.zeros((4, SEQ, D), np.float32)
    for core in range(8):
        b, half = core // 2, core % 2
        o = results[core]["outT"].reshape(D, -1)[:, :T].T
        idx = _local_to_global(half, T)
        out[b, idx] = o
    return out


def kernel(**inputs):
    in_maps = prepare_inputs(inputs)
    nc, stack = build_program(None)
    with stack:
        res = run_bass_kernel_spmd(nc, in_maps, core_ids=list(range(8)))
    return assemble(res.results, 1024)
```

```python
import numpy as np
from contextlib import ExitStack
import concourse.bass as bass
import concourse.mybir as mybir
from concourse.bass_utils import run_bass_kernel_spmd

F32 = mybir.dt.float32
BF16 = mybir.dt.bfloat16
AF = mybir.ActivationFunctionType
ALU = mybir.AluOpType

D = 2048
NCH = 16
DFF = 5632
NF = 44
DEPTH = 4
SEQ = 2048
GW = 64
EPS = 1e-6
NEG = -30000.0
POOL_W = (2, 4, 8, 16)

T_KV = {0: 1664, 2: 1344}
T_OUT = {0: 1352, 1: 1344, 2: 1032, 3: 1024}
T_IN = {0: 1352, 1: 1352, 2: 1344, 3: 1032}
TX = 1352
TX0 = 1664
ARENA_BYTES = 209920

VC = {}
_c = 0
for _i in range(DEPTH):
    for _nm in ("mix_g", "ffn_g", "ple_g", "ple_b"):
        VC[(_nm, _i)] = _c
        _c += 16
for _j in range(2):
    for _nm in ("bq", "bk", "pscale"):
        VC[(_nm, _j)] = _c
        _c += 16
VC["final_g"] = _c
_c += 16
VC["cA"] = _c
VC["cB"] = _c + 1
_c += 2
VC["corr"] = _c
_c += 32
VC["eps"] = _c
_c += 1
VC["hmask"] = _c
_c += 4
NVEC = _c


def blocks(T, maxn=512):
    nb = -(-T // maxn)
    base = -(-T // nb)
    base = -(-base // 8) * 8
    out = []
    s = 0
    while s < T:
        n = min(base, T - s)
        out.append((s, n))
        s += n
    return out


class Slot:
    def __init__(self, ap=None, **kw):
        self.ap = ap
        self.w = []
        self.r = []
        self.__dict__.update(kw)

    def wwaits(self):
        return list(self.r) + list(self.w)

    def set_w(self, *toks):
        self.w = [t for t in toks if t is not None]
        self.r = []

    def add_w(self, tok):
        self.w.append(tok)

    def rwaits(self):
        return list(self.w)

    def add_r(self, tok):
        if tok is not None:
            self.r.append(tok)


class Ring:
    def __init__(self, slots):
        self.slots = slots
        self.i = 0

    def next(self):
        s = self.slots[self.i % len(self.slots)]
        self.i += 1
        return s


class Eng:
    def __init__(self, prog, name, semi):
        self.prog = prog
        self.name = name
        self.semi = semi
        self.ops = []
        self.waited = {}

    def do(self, fn, waits=(), sig=False, inc=None, nobar=False):
        ws = []
        allw = list(waits)
        if not nobar:
            allw += self.prog.barrier
        for t in allw:
            if t is None:
                continue
            s, v = t
            if self.waited.get(s, 0) >= v:
                continue
            self.waited[s] = v
            ws.append((s, v))
        op = {"fn": fn, "ws": ws, "inc": None}
        self.ops.append(op)
        if inc is not None:
            return self._inc(op, inc)
        if sig:
            return self._inc(op, (self.semi, 1))
        return None

    def _inc(self, op, inc):
        assert op["inc"] is None
        op["inc"] = inc
        self.prog.semcount[inc[0]] += inc[1]
        op["tok"] = (inc[0], self.prog.semcount[inc[0]])
        return op["tok"]

    def last_token(self):
        if not self.ops:
            return None
        op = self.ops[-1]
        if op["inc"] is None:
            return self._inc(op, (self.semi, 1))
        return op["tok"]


class Prog:
    def __init__(self, nc, stack):
        self.nc = nc
        self.stack = stack
        self.sems = []
        self.semnames = {}
        self.semcount = []
        self.barrier = []
        self.pe = Eng(self, "pe", self.new_sem("pe"))
        self.act = Eng(self, "act", self.new_sem("act"))
        self.dve = Eng(self, "dve", self.new_sem("dve"))
        self.sp = Eng(self, "sp", None)
        self.pool = Eng(self, "pool", None)

    def new_sem(self, name):
        if name in self.semnames:
            return self.semnames[name]
        self.semnames[name] = len(self.sems)
        s = self.stack.enter_context(self.nc.semaphore(name))
        self.sems.append(s)
        self.semcount.append(0)
        return len(self.sems) - 1

    def phase_barrier(self):
        toks = [e.last_token() for e in (self.pe, self.act, self.dve)]
        self.barrier = [t for t in toks if t is not None]

    def replay(self):
        sems = self.sems

        def run(E, e):
            for op in E.ops:
                for (s, v) in op["ws"]:
                    e.wait_ge(sems[s], v)
                ins = op["fn"](e)
                if op["inc"] is not None:
                    ins.then_inc(sems[op["inc"][0]], op["inc"][1])

        with self.nc.Block() as block:
            @block.tensor
            def _(e):
                run(self.pe, e)

            @block.scalar
            def _(e):
                run(self.act, e)

            @block.vector
            def _(e):
                run(self.dve, e)

            @block.sync
            def _(e):
                run(self.sp, e)

            @block.gpsimd
            def _(e):
                run(self.pool, e)


class Arena:
    def __init__(self, ap, nbytes):
        self.ap = ap
        self.nbytes = nbytes

    def view(self, off, shape, dtype):
        n = int(np.prod(shape))
        assert off % 4 == 0
        if dtype == BF16:
            assert off + 2 * n <= self.nbytes, (off, shape)
            v = self.ap[:, off // 2: off // 2 + n]
        else:
            assert off + 4 * n <= self.nbytes, (off, shape)
            v = self.ap[:, off // 2: off // 2 + 2 * n].bitcast(F32)
        if len(shape) == 2:
            v = v.rearrange("p (a b) -> p a b", b=shape[1])
        elif len(shape) == 3:
            v = v.rearrange("p (a b c) -> p a b c", b=shape[1], c=shape[2])
        return v


class Alloc:
    def __init__(self, arena, base):
        self.arena = arena
        self.off = base

    def take(self, shape, dtype):
        n = int(np.prod(shape)) * (2 if dtype == BF16 else 4)
        n = -(-n // 64) * 64
        v = self.arena.view(self.off, shape, dtype)
        self.off += n
        assert self.off <= self.arena.nbytes - 4 * 4096, ("arena overflow", self.off)
        return v


def build_program(stop_after=None):
    nc = bass.Bass("TRN2", target_bir_lowering=False)
    stack = ExitStack()
    dt = nc.dram_tensor
    xT = dt("xT", [NCH, 128, TX0], F32, kind="ExternalInput").ap()
    pT = dt("pT", [DEPTH, 2, 128, TX], F32, kind="ExternalInput").ap()
    vecs_d = dt("vecs", [128, NVEC], F32, kind="ExternalInput").ap()
    bv_d = dt("bv", [2, 128, D], F32, kind="ExternalInput").ap()
    ident_d = dt("ident", [128, 128], F32, kind="ExternalInput").ap()
    wqkv_d = dt("wqkv", [2, 16, 3, 128, 2048], F32, kind="ExternalInput").ap()
    wo_d = dt("wo", [2, 8, 128, 2, D], F32, kind="ExternalInput").ap()
    tab0_d = dt("tab0", [2, 16, 128, 4 * 4 * 128], F32, kind="ExternalInput").ap()
    tab1_d = dt("tab1", [2, 16, 128, 4 * 4 * 128], F32, kind="ExternalInput").ap()
    tab2_d = dt("tab2", [2, 16, 128, 4 * 5 * 128], F32, kind="ExternalInput").ap()
    wpool_d = dt("wpool", [2, 128, 16 * 512], F32, kind="ExternalInput").ap()
    wgate_d = dt("wgate", [DEPTH, NF, 128, 2048], F32, kind="ExternalInput").ap()
    wup_d = dt("wup", [DEPTH, NF, 128, 2048], F32, kind="ExternalInput").ap()
    wdown_d = dt("wdown", [DEPTH, 11, 16, 128, 4 * 128], F32, kind="ExternalInput").ap()
    wpg_d = dt("wpg", [DEPTH, 16, 128, 2048], F32, kind="ExternalInput").ap()
    wpp_d = dt("wpp", [DEPTH, 128, 2 * D], F32, kind="ExternalInput").ap()
    T_DUMP = 1024 if stop_after is None else TX
    outT = dt("outT", [NCH, 128, T_DUMP], F32, kind="ExternalOutput").ap()

    P = Prog(nc, stack)
    pe, act, dve, sp, pool = P.pe, P.act, P.dve, P.sp, P.pool
    arena_t = stack.enter_context(nc.sbuf_tensor("arena", [128, ARENA_BYTES // 2], BF16))
    arena = Arena(arena_t, ARENA_BYTES)
    vecs = stack.enter_context(nc.sbuf_tensor("vecs_sb", [128, NVEC], F32))
    ones = stack.enter_context(nc.sbuf_tensor("ones_sb", [128, 128], BF16))
    ident = stack.enter_context(nc.sbuf_tensor("ident_sb", [128, 128], BF16))
    bqs = stack.enter_context(nc.sbuf_tensor("bqs_sb", [128, 16], F32))
    sbig = stack.enter_context(nc.psum_tensor("sbig", [128, 2048], F32))
    banks = [sbig[:, i * 512:(i + 1) * 512] for i in range(4)]
    banks += [stack.enter_context(nc.psum_tensor(f"bank{i}", [128, 512], F32))[:, :] for i in range(4, 8)]
    sview = sbig[:, :].rearrange("p (a s b) -> p a s b", s=4, b=128)

    XSZ = NCH * TX * 4
    X = arena.view(0, (NCH, TX), F32)
    NWT = 4
    WT_BASE = ARENA_BYTES - NWT * 4096
    wt_slots = [Slot(arena.view(WT_BASE + i * 4096, (2048,), BF16), sem=P.new_sem(f"wt{i}")) for i in range(NWT)]
    wt_ring = Ring(wt_slots)

    def vcol(key, k=0, n=1):
        c = VC[key] + k
        return vecs[:, c:c + n]

    def dma_in(E, slot, out_ap, in_ap, nobar=False, extra_waits=()):
        tok = E.do(lambda e, o=out_ap, i=in_ap: e.dma_start(out=o, in_=i),
                   waits=slot.wwaits() + list(extra_waits), inc=(slot.sem, 16), nobar=nobar)
        slot.set_w(tok)
        return tok

    def load_wtile(src_ap):
        s = wt_ring.next()
        dma_in(pool, s, s.ap, src_ap, nobar=True)
        return s

    def mm(out, lhsT, rhs, start, stop, waits=(), sig=False, tp=None):
        if tp is None:
            fn = lambda e: e.matmul(out, lhsT=lhsT, rhs=rhs, start=start, stop=stop)
        else:
            fn = lambda e: e.matmul(out, lhsT=lhsT, rhs=rhs, start=start, stop=stop, tile_position=tp)
        return pe.do(fn, waits=waits, sig=sig)

    def bank_ring(idxs):
        return Ring([Slot(banks[i]) for i in idxs])

    vec_slot = Slot(sem=P.new_sem("vecs"))
    dma_in(sp, vec_slot, vecs[:, :], vecs_d[:, :])
    x_slot = Slot(sem=P.new_sem("xload"))
    xtoks = []
    for k in range(NCH):
        xtoks.append(sp.do(lambda e, k=k: e.dma_start(out=X[:, k, :], in_=xT[k, :, 0:TX]),
                           inc=(x_slot.sem, 16)))
    x_ready = [xtoks[-1]]
    ones_tok = dve.do(lambda e: e.memset(ones[:, :], 1.0), sig=True)
    id_slot = Slot(sem=P.new_sem("ident"))
    dma_in(pool, id_slot, ident[:, :], ident_d[:, :])
    const_ready = [ones_tok] + vec_slot.rwaits() + id_slot.rwaits()

    def norm_phase(T, gkey, H, al, src=None, tok_off=0, keep_rstd=False, out_f32_X=False, extra_waits=(), ring=None,
                   blk_toks=None):
        if src is None:
            src = lambda k, s, n: X[:, k, s:s + n]
        SQ = Ring([Slot(al.take((512,), BF16)) for _ in range(3)])
        TMP = Slot(al.take((512,), F32))
        RSTD = al.take((T,), F32)
        if ring is None:
            ring = bank_ring([6, 7])
        ew = list(extra_waits) + const_ready + x_ready
        for (s, n) in blocks(T):
            ps = ring.next()
            for k in range(NCH):
                sq = SQ.next()
                t = act.do(lambda e, o=sq.ap[:, 0:n], i=src(k, s, n): e.activation(out=o, in_=i, func=AF.Square),
                           waits=sq.wwaits() + ew, sig=True)
                sq.set_w(t)
                t = mm(ps.ap[:, 0:n], ones[:, :], sq.ap[:, 0:n], k == 0, k == NCH - 1,
                       waits=sq.rwaits() + (ps.wwaits() if k == 0 else []) + ew, sig=True)
                sq.add_r(t)
            ps.set_w(t)
            t = act.do(lambda e, o=TMP.ap[:, 0:n], i=ps.ap[:, 0:n]: e.activation(
                out=o, in_=i, func=AF.Sqrt, bias=vcol("eps"), scale=1.0 / D),
                waits=ps.rwaits() + TMP.wwaits(), sig=True)
            ps.add_r(t)
            TMP.set_w(t)
            t = dve.do(lambda e, o=RSTD[:, s:s + n], i=TMP.ap[:, 0:n]: e.reciprocal(out=o, in_=i),
                       waits=TMP.rwaits(), sig=True)
            TMP.add_r(t)
            if H is not None:
                for k in range(NCH):
                    o = X[:, k, s:s + n] if out_f32_X else H[:, k, tok_off + s: tok_off + s + n]
                    t2 = dve.do(lambda e, o=o, i=src(k, s, n), g=vcol(gkey, k), r=RSTD[:, s:s + n]:
                                e.scalar_tensor_tensor(out=o, in0=i, scalar=g, in1=r, op0=ALU.mult, op1=ALU.mult),
                                waits=[t] + ew, sig=(k == NCH - 1))
                if blk_toks is not None:
                    blk_toks.append(t2)
        return RSTD, t

    def attention_layer(i):
        j = i // 2
        Tkv, Tq = T_KV[i], T_OUT[i]
        H = arena.view(XSZ, (NCH, Tkv), BF16)
        base = XSZ + NCH * Tkv * 2
        P.phase_barrier()
        al = Alloc(arena, base)
        if i == 0:
            norm_phase(TX, ("mix_g", i), H, al)
            nh = Tkv - TX
            XH = al.take((NCH, nh), F32)
            xh_slot = Slot(sem=P.new_sem("xh"))
            for k in range(NCH):
                tk = sp.do(lambda e, k=k: e.dma_start(out=XH[:, k, :], in_=xT[k, :, TX:Tkv]),
                           inc=(xh_slot.sem, 16))
            al2 = Alloc(arena, al.off)
            norm_phase(nh, ("mix_g", i), H, al2, src=lambda k, s, n: XH[:, k, s:s + n], tok_off=TX,
                       extra_waits=[tk])
        else:
            norm_phase(Tkv, ("mix_g", i), H, al)
        if stop_after == (i, "n"):
            return
        P.phase_barrier()
        al = Alloc(arena, base)
        nkc = Tkv // 128 if Tkv % 128 == 0 else Tkv // 128 + 1
        scale = 32 ** -0.5
        QKV = []
        for b in range(2):
            QKV.append(dict(Q=Slot(al.take((Tq,), BF16)), K=Slot(al.take((Tkv,), BF16)),
                            V=Slot(al.take((nkc, 128), BF16))))
        TB = [Slot(al.take((4 * 512,), BF16), sem=P.new_sem("tb0")),
              Slot(al.take((4 * 512,), BF16), sem=P.new_sem("tb1")),
              Slot(al.take((5 * 512,), BF16), sem=P.new_sem("tb2"))]
        tbq = dve.do(lambda e: e.tensor_scalar(out=bqs[:, :], in0=vcol(("bq", j), 0, 16), scalar1=scale, scalar2=None,
                                               op0=ALU.mult), waits=const_ready, sig=True)
        tabs_d = [tab0_d, tab1_d, tab2_d]
        OT = [Slot(al.take((2, Tq), BF16))]
        WO = Slot(al.take((2, D), BF16), sem=P.new_sem("wo"))
        PR = Ring([Slot(al.take((4, 128), BF16)) for _ in range(3)])
        RC = Ring([Slot(al.take((128,), F32)) for _ in range(2)])
        BVB = Ring([Slot(al.take((128,), F32), sem=P.new_sem(f"bvb{q}")) for q in range(2)])
        s_ring = bank_ring([0, 1, 4, 5])
        o_ring = Ring([Slot((banks[2], banks[3]))])
        g_ring = bank_ring([6, 7])
        wo_ring = Ring(s_ring.slots + g_ring.slots)
        QB = Ring([Slot(al.take((4, 128), BF16)) for _ in range(2)])
        for qbs in QB.slots:
            qbs.set_w(dve.do(lambda e, o=qbs.ap: e.memset(o, 0.0), sig=True))
        qblocks = blocks(Tq)
        kblocks = blocks(Tkv)
        wtiles = {}
        bvbs = {}

        def dma_w(g):
            wtiles[g] = [load_wtile(wqkv_d[j, g, s]) for s in range(3)]
            sl = BVB.next()
            dma_in(sp, sl, sl.ap, bv_d[j, :, g * 128:(g + 1) * 128])
            bvbs[g] = sl

        def dma_tab(g):
            for t in range(3):
                sl = TB[t]
                dma_in(pool, sl, sl.ap, tabs_d[t][j, g])

        def qkv(g):
            buf = QKV[g % 2]
            wq, wk, wv = wtiles[g]
            for (dst, w, bkey, blks) in ((buf["Q"], wq, None, qblocks), (buf["K"], wk, "bk", kblocks)):
                first = True
                toks = []
                for (s, n) in blks:
                    ps = g_ring.next()
                    for k in range(NCH):
                        t = mm(ps.ap[:, 0:n], w.ap[:, k * 128:(k + 1) * 128], H[:, k, s:s + n], k == 0, k == NCH - 1,
                               waits=(ps.wwaits() + w.rwaits()) if k == 0 else (), sig=(k == NCH - 1))
                    ps.set_w(t)
                    bb = bqs[:, g:g + 1] if bkey is None else vcol((bkey, j), g)
                    sc = scale if bkey is None else 1.0
                    t2 = act.do(lambda e, o=dst.ap[:, s:s + n], i=ps.ap[:, 0:n], b=bb, sc=sc:
                                e.activation(out=o, in_=i, func=AF.Identity, bias=b, scale=sc),
                                waits=ps.rwaits() + (dst.wwaits() if first else []) + [tbq], sig=True)
                    first = False
                    ps.add_r(t2)
                    toks.append(t2)
                w.add_r(t)
                dst.set_w(toks[-1])
            dstv = buf["V"]
            bvb = bvbs[g]
            first = True
            for c0 in range(0, nkc, 4):
                ps = g_ring.next()
                ncs = min(4, nkc - c0)
                for cc in range(ncs):
                    c = c0 + cc
                    nk = min(128, Tkv - c * 128)
                    for k in range(NCH):
                        t = mm(ps.ap[0:nk, cc * 128:(cc + 1) * 128], H[:, k, c * 128:c * 128 + nk],
                               wv.ap[:, k * 128:(k + 1) * 128], k == 0, k == NCH - 1,
                               waits=(ps.wwaits() + wv.rwaits()) if (k == 0 and cc == 0) else (),
                               sig=(k == NCH - 1 and cc == ncs - 1))
                ps.set_w(t)
                for cc in range(ncs):
                    c = c0 + cc
                    nk = min(128, Tkv - c * 128)
                    t2 = dve.do(lambda e, o=dstv.ap[0:nk, c, :], a=ps.ap[0:nk, cc * 128:(cc + 1) * 128], b=bvb.ap[0:nk, :]:
                                e.tensor_tensor(out=o, in0=a, in1=b, op=ALU.add),
                                waits=ps.rwaits() + bvb.rwaits() + (dstv.wwaits() if first else []), sig=(cc == ncs - 1))
                    first = False
                ps.add_r(t2)
            wv.add_r(t)
            bvb.add_r(t2)
            dstv.set_w(t2)

        def attend(g):
            buf = QKV[g % 2]
            Qs, Ks, Vs = buf["Q"], buf["K"], buf["V"]
            ot = OT[0]
            gi = g % 2
            npair = -(-Tq // 128)
            last_pv = None
            q_last_tok = [None]
            for pj in range(npair):
                q0 = pj * 128
                nq = min(128, Tq - q0)
                if pj < 2:
                    tb, cs, ncz = TB[pj], 0, 4
                else:
                    tb, cs, ncz = TB[2], pj - 2, 5
                ob = o_ring.next()
                chunks = []
                for c in range(ncz):
                    kc = cs + c
                    nk = min(128, Tkv - kc * 128)
                    assert nk in (64, 128)
                    chunks.append((c, kc, nk))
                qb = QB.next()
                qb_last = [None]
                for hh in range(4):
                    tq = dve.do(lambda e, o=qb.ap[:, hh, 0:nq], i=Qs.ap[:, q0:q0 + nq], m=vcol("hmask", hh):
                                e.tensor_scalar(out=o, in0=i, scalar1=m, scalar2=None, op0=ALU.mult),
                                waits=(qb.wwaits() + Qs.rwaits()) if hh == 0 else (), sig=(hh == 3))
                qb.set_w(tq)
                q_last_tok[0] = tq

                def issue_s(c, kc, nk):
                    sb = s_ring.next()
                    sbv = sb.ap.rearrange("p (a b) -> p a b", b=128)[0:nk, :, 0:nq]
                    t = mm(sb.ap[0:nk, :], Ks.ap[:, kc * 128:kc * 128 + nk], qb.ap.rearrange("p a b -> p (a b)"), True, False,
                           waits=sb.wwaits() + qb.rwaits() + Ks.rwaits())
                    t = mm(sb.ap[0:nk, :], ident[0:nk, 0:nk], tb.ap[0:nk, c * 512:(c + 1) * 512], False, True,
                           waits=tb.rwaits(), sig=True)
                    qb_last[0] = t
                    sb.set_w(t)
                    pr = PR.next()
                    t2 = act.do(lambda e, o=pr.ap[0:nk, :, 0:nq], i=sbv: e.activation(out=o, in_=i, func=AF.Exp),
                                waits=sb.rwaits() + pr.wwaits(), sig=True)
                    sb.add_r(t2)
                    pr.set_w(t2)
                    return pr

                def issue_pv(c, kc, nk, pr):
                    nonlocal last_pv
                    for hh in range(4):
                        w = (pr.rwaits() + Vs.rwaits() + (ob.wwaits() if c == 0 else [])) if hh == 0 else ()
                        mm(ob.ap[0][32 * hh:32 * hh + 32, 0:nq], Vs.ap[0:nk, kc, 32 * hh:32 * hh + 32], pr.ap[0:nk, hh, 0:nq],
                           c == 0, c == ncz - 1, waits=w, tp=(0, 32 * hh))
                    for hh in range(4):
                        t = mm(ob.ap[1][32 * hh:32 * hh + 32, 0:nq], ones[0:nk, 0:32], pr.ap[0:nk, hh, 0:nq],
                               c == 0, c == ncz - 1, sig=(hh == 3), tp=(0, 32 * hh))
                    pr.add_r(t)
                    last_pv = t
                    return t

                prs = {}
                LA = 2
                for c in range(min(LA, ncz)):
                    prs[c] = issue_s(*chunks[c])
                for c in range(ncz):
                    if c + LA < ncz:
                        prs[c + LA] = issue_s(*chunks[c + LA])
                    t = issue_pv(*chunks[c], prs[c])
                ob.set_w(t)
                qb.add_r(qb_last[0])
                rc = RC.next()
                t1 = dve.do(lambda e, o=rc.ap[:, 0:nq], i=ob.ap[1][:, 0:nq]: e.reciprocal(out=o, in_=i),
                            waits=ob.rwaits() + rc.wwaits(), sig=True)
                t2 = dve.do(lambda e, o=ot.ap[:, gi, q0:q0 + nq], a=ob.ap[0][:, 0:nq], b=rc.ap[:, 0:nq]:
                            e.tensor_tensor(out=o, in0=a, in1=b, op=ALU.mult),
                            waits=[t1] + (ot.wwaits() if (pj == 0 and gi == 0) else []), sig=True)
                rc.set_w(t2)
                ob.add_r(t2)
            Qs.add_r(q_last_tok[0])
            Ks.add_r(last_pv)
            Vs.add_r(last_pv)
            if gi == 0:
                ot.set_w(t2)
            else:
                ot.add_w(t2)

        def wo_dma(s):
            flat = WO.ap.rearrange("p a b -> p (a b)")
            dma_in(pool, WO, flat, wo_d[j, s].rearrange("p a b -> p (a b)"))

        def wo_set(s):
            ot = OT[0]
            for dc in range(NCH):
                for (s0, n) in qblocks:
                    ps = wo_ring.next()
                    for gi in range(2):
                        t = mm(ps.ap[:, 0:n], WO.ap[:, gi, dc * 128:(dc + 1) * 128], ot.ap[:, gi, s0:s0 + n], gi == 0, gi == 1,
                               waits=(ps.wwaits() + WO.rwaits() + ot.rwaits()) if gi == 0 else (), sig=(gi == 1))
                    ps.set_w(t)
                    t2 = dve.do(lambda e, o=X[:, dc, s0:s0 + n], a=ps.ap[:, 0:n]:
                                e.tensor_tensor(out=o, in0=a, in1=o, op=ALU.add),
                                waits=ps.rwaits() + x_ready, sig=True)
                    ps.add_r(t2)
            WO.add_r(t)
            ot.add_r(t)

        def attend_body(g):
            attend(g)
            tok = pe.last_token()
            for t in range(3):
                TB[t].add_r(tok)

        dma_w(0)
        dma_tab(0)
        qkv(0)
        if stop_after == (i, "q"):
            return
        if stop_after == (i, "a"):
            attend_body(0)
            return
        if 1 < 16:
            dma_w(1)
        for g in range(16):
            if g + 1 < 16:
                qkv(g + 1)
            if g + 2 < 16:
                dma_w(g + 2)
            if g % 2 == 1:
                wo_dma(g // 2)
            attend_body(g)
            if g + 1 < 16:
                dma_tab(g + 1)
            if g % 2 == 1:
                wo_set(g // 2)

    def pool_layer(i):
        j = i // 2
        Tin, Tout = T_IN[i], T_OUT[i]
        H = arena.view(XSZ, (NCH, Tin), BF16)
        base = XSZ + NCH * Tin * 2
        P.phase_barrier()
        al = Alloc(arena, base)
        WP = Slot(al.take((16, 512), BF16), sem=P.new_sem("wp"))
        dma_in(pool, WP, WP.ap.rearrange("p a b -> p (a b)"), wpool_d[j])
        RSTD, rtok = norm_phase(Tin, None, None, al)
        PADL = 8
        HF = [Slot(al.take((PADL + Tin,), F32)) for _ in range(2)]
        PA = al.take((PADL + Tin,), F32)
        PB = al.take((PADL + Tin,), F32)
        for hf in HF:
            t = dve.do(lambda e, o=hf.ap[:, 0:PADL]: e.memset(o, 0.0), sig=True)
            hf.set_w(t)
        ring = bank_ring([0, 1, 2, 3])
        oblocks = blocks(Tout)
        prev_tl = rtok
        for k in range(NCH):
            grp = k // 4
            w = POOL_W[grp]
            hf = HF[k % 2]
            h = hf.ap
            tl = dve.do(lambda e, o=h[:, PADL:PADL + Tin], x=X[:, k, 0:Tin], g=vcol(("mix_g", i), k), r=RSTD[:, 0:Tin]:
                        e.scalar_tensor_tensor(out=o, in0=x, scalar=g, in1=r, op0=ALU.mult, op1=ALU.mult),
                        waits=hf.wwaits() + const_ready + [prev_tl], sig=True)
            L = PADL + Tin
            cur = h
            curw = 1
            bufs = [PA, PB]
            bi = 0
            while curw < w:
                nxt = bufs[bi]
                bi ^= 1
                n = L - 2 * curw + 1
                tl = dve.do(lambda e, o=nxt[:, 0:n], a=cur[:, 0:n], b=cur[:, curw:curw + n]:
                            e.tensor_tensor(out=o, in0=a, in1=b, op=ALU.add), waits=[tl], sig=True)
                cur = nxt
                curw *= 2
            U = bufs[bi]
            sA = PADL - w // 2
            tl = dve.do(lambda e, o=U[:, 0:Tout], a=cur[:, sA:sA + Tout], c=vcol("cA"):
                        e.tensor_scalar(out=o, in0=a, scalar1=c, scalar2=None, op0=ALU.mult), waits=[tl], sig=True)
            tl = dve.do(lambda e, o=U[:, 0:Tout], a=cur[:, sA + 1:sA + 1 + Tout], c=vcol("cB"):
                        e.scalar_tensor_tensor(out=o, in0=a, scalar=c, in1=o, op0=ALU.mult, op1=ALU.add),
                        waits=[tl], sig=True)
            tl = dve.do(lambda e, o=U[:, 0:8], c=vcol("corr", grp * 8, 8):
                        e.tensor_tensor(out=o, in0=o, in1=c, op=ALU.mult), waits=[tl], sig=True)
            tl = dve.do(lambda e, o=H[:, k, 0:Tout], a=U[:, 0:Tout], b=h[:, PADL:PADL + Tout], iw=1.0 / w:
                        e.scalar_tensor_tensor(out=o, in0=a, scalar=iw, in1=b, op0=ALU.mult, op1=ALU.subtract),
                        waits=[tl], sig=True)
            hf.set_w(tl)
            prev_tl = tl
            if k % 4 == 3:
                for ec in range(4):
                    for (s0, n) in oblocks:
                        ps = ring.next()
                        for c in range(4):
                            t = mm(ps.ap[:, 0:n], WP.ap[:, grp * 4 + c, ec * 128:(ec + 1) * 128], H[:, grp * 4 + c, s0:s0 + n],
                                   c == 0, c == 3, waits=(ps.wwaits() + WP.rwaits() + [tl]) if c == 0 else (), sig=(c == 3))
                        ps.set_w(t)
                        ko = grp * 4 + ec
                        t2 = dve.do(lambda e, o=X[:, ko, s0:s0 + n], a=ps.ap[:, 0:n], sc=vcol(("pscale", j), ko):
                                    e.scalar_tensor_tensor(out=o, in0=a, scalar=sc, in1=o, op0=ALU.mult, op1=ALU.add),
                                    waits=ps.rwaits(), sig=True)
                        ps.add_r(t2)

    def ffn_layer(i):
        T = T_OUT[i]
        H = arena.view(XSZ, (NCH, T), BF16)
        base = XSZ + NCH * T * 2
        P.phase_barrier()
        al = Alloc(arena, base)
        G = 4
        NG = NF // G
        ACTB = [Slot(al.take((G, T), BF16)) for _ in range(2)]
        SG = Ring([Slot(al.take((512,), F32)) for _ in range(2)])
        WD = Ring([Slot(al.take((G, 128), BF16), sem=P.new_sem(f"wd{q}")) for q in range(6)])
        gu_ring = Ring([Slot((banks[0], banks[1])), Slot((banks[2], banks[3]))])
        nring = bank_ring([6, 7])
        dn_ring = Ring([Slot(banks[4]), Slot(banks[5])] + nring.slots)
        tblocks = blocks(T)
        hb = []
        norm_phase(T, ("ffn_g", i), H, al, ring=nring, blk_toks=hb)

        def gate_up(gi):
            ab = ACTB[gi % 2]
            first = True
            for fi in range(G):
                f = gi * G + fi
                wg = load_wtile(wgate_d[i, f])
                wu = load_wtile(wup_d[i, f])
                for bi, (s, n) in enumerate(tblocks):
                    pp = gu_ring.next()
                    bg, bu = pp.ap
                    for (bk, w) in ((bg, wg), (bu, wu)):
                        for k in range(NCH):
                            t = mm(bk[:, 0:n], w.ap[:, k * 128:(k + 1) * 128], H[:, k, s:s + n], k == 0, k == NCH - 1,
                                   waits=(pp.wwaits() + w.rwaits() + [hb[bi]]) if k == 0 else (), sig=(k == NCH - 1))
                    pp.set_w(t)
                    sg = SG.next()
                    t2 = act.do(lambda e, o=sg.ap[:, 0:n], a=bg[:, 0:n]: e.activation(out=o, in_=a, func=AF.Silu),
                                waits=pp.rwaits() + sg.wwaits(), sig=True)
                    t3 = dve.do(lambda e, o=ab.ap[:, fi, s:s + n], a=sg.ap[:, 0:n], b=bu[:, 0:n]:
                                e.tensor_tensor(out=o, in0=a, in1=b, op=ALU.mult),
                                waits=[t2] + (ab.wwaits() if first else []), sig=True)
                    first = False
                    sg.set_w(t3)
                    pp.add_r(t3)
                wg.add_r(t)
                wu.add_r(t)
            ab.set_w(t3)

        def down(gi):
            ab = ACTB[gi % 2]
            for dc in range(NCH):
                wd = WD.next()
                dma_in(pool, wd, wd.ap.rearrange("p a b -> p (a b)"), wdown_d[i, gi, dc])
                for (s, n) in tblocks:
                    ps = dn_ring.next()
                    for fi in range(G):
                        t = mm(ps.ap[:, 0:n], wd.ap[:, fi, :], ab.ap[:, fi, s:s + n], fi == 0, fi == G - 1,
                               waits=(ps.wwaits() + wd.rwaits() + ab.rwaits()) if fi == 0 else (), sig=(fi == G - 1))
                    ps.set_w(t)
                    t2 = dve.do(lambda e, o=X[:, dc, s:s + n], a=ps.ap[:, 0:n]:
                                e.tensor_tensor(out=o, in0=a, in1=o, op=ALU.add), waits=ps.rwaits(), sig=True)
                    ps.add_r(t2)
                wd.add_r(t)
            ab.add_r(t)

        gate_up(0)
        for gi in range(NG):
            if gi + 1 < NG:
                gate_up(gi + 1)
            down(gi)

    def ple_layer(i):
        T = T_OUT[i]
        H = arena.view(XSZ, (NCH, T), BF16)
        base = XSZ + NCH * T * 2
        P.phase_barrier()
        al = Alloc(arena, base)
        hb = []
        PT = Slot(al.take((2, T), BF16), sem=P.new_sem("pt"))
        WPP = Slot(al.take((2, D), BF16), sem=P.new_sem("wpp"))
        SG = Ring([Slot(al.take((512,), F32)) for _ in range(2)])
        for c in range(2):
            tk = pool.do(lambda e, c=c: e.dma_start(out=PT.ap[:, c, :], in_=pT[i, c, :, 0:T]), inc=(PT.sem, 16))
        PT.set_w(tk)
        dma_in(pool, WPP, WPP.ap.rearrange("p a b -> p (a b)"), wpp_d[i])
        norm_phase(T, ("ple_g", i), H, al, blk_toks=hb)
        ring = Ring([Slot((banks[0], banks[1])), Slot((banks[2], banks[3])), Slot((banks[4], banks[5]))])
        for dc in range(NCH):
            w = load_wtile(wpg_d[i, dc])
            for bi, (s, n) in enumerate(blocks(T)):
                pp = ring.next()
                bg, bp = pp.ap
                for k in range(NCH):
                    t = mm(bg[:, 0:n], w.ap[:, k * 128:(k + 1) * 128], H[:, k, s:s + n], k == 0, k == NCH - 1,
                           waits=(pp.wwaits() + w.rwaits() + [hb[bi]]) if k == 0 else ())
                for c in range(2):
                    t = mm(bp[:, 0:n], WPP.ap[:, c, dc * 128:(dc + 1) * 128], PT.ap[:, c, s:s + n], c == 0, c == 1,
                           waits=(PT.rwaits() + WPP.rwaits()) if c == 0 else (), sig=(c == 1))
                pp.set_w(t)
                sg = SG.next()
                t2 = act.do(lambda e, o=sg.ap[:, 0:n], a=bg[:, 0:n], b=vcol(("ple_b", i), dc):
                            e.activation(out=o, in_=a, func=AF.Sigmoid, bias=b, scale=1.0),
                            waits=pp.rwaits() + sg.wwaits(), sig=True)
                t3 = dve.do(lambda e, o=sg.ap[:, 0:n], b=bp[:, 0:n]: e.tensor_tensor(out=o, in0=o, in1=b, op=ALU.mult),
                            waits=[t2], sig=True)
                pp.add_r(t3)
                t4 = dve.do(lambda e, o=X[:, dc, s:s + n], a=sg.ap[:, 0:n]: e.tensor_tensor(out=o, in0=a, in1=o, op=ALU.add),
                            waits=[t3], sig=True)
                sg.set_w(t4)
            w.add_r(t)

    done = False
    for i in range(DEPTH):
        if i % 2 == 0:
            attention_layer(i)
        else:
            pool_layer(i)
        if stop_after is not None and stop_after[0] == i and stop_after[1] in "mnqa":
            done = True
            break
        ffn_layer(i)
        if stop_after == (i, "f"):
            done = True
            break
        ple_layer(i)
        if stop_after == (i, "p"):
            done = True
            break
    if not done:
        P.phase_barrier()
        al = Alloc(arena, XSZ)
        Hdummy = arena.view(XSZ, (NCH, 1024), BF16)
        norm_phase(1024, "final_g", Hdummy, al, out_f32_X=True)
    P.phase_barrier()
    out_sem = P.new_sem("out")
    for k in range(NCH):
        tk = sp.do(lambda e, k=k: e.dma_start(out=outT[k, :, :], in_=X[:, k, 0:T_DUMP]), inc=(out_sem, 16))
    sp.do(lambda e: e.wait_ge(P.sems[out_sem], P.semcount[out_sem]))
    P.replay()
    return nc, stack


def _local_to_global(half, n):
    l = np.arange(n)
    return l if half == 0 else (SEQ - 1 - l)


def _bias_tables(rpb_j, half):
    outs = []
    for (qrow0, krow0, ncz) in ((0, 0, 4), (2, 0, 4), (8, 4, 5)):
        p = np.arange(128)
        c = np.arange(ncz)
        kl_row = krow0 + 2 * c[:, None] + p[None, :] // 64
        kl_col = np.broadcast_to(p[None, :] % 64, kl_row.shape)
        q = np.arange(128)
        ql_row = qrow0 + q // 64
        ql_col = q % 64
        if half == 0:
            rk, ck, rq, cq = kl_row, kl_col, ql_row, ql_col
        else:
            rk, ck, rq, cq = 31 - kl_row, 63 - kl_col, 31 - ql_row, 63 - ql_col
        rk = rk[:, :, None]
        ck = ck[:, :, None]
        rq = rq[None, None, :]
        cq = cq[None, None, :]
        r_start = np.clip(rq - 4, 0, 24)
        c_start = np.clip(cq - 8, 0, 48)
        valid = (rk >= r_start) & (rk < r_start + 8) & (ck >= c_start) & (ck < c_start + 16)
        dr = np.clip(rk - rq + 7, 0, 14)
        dc = np.clip(ck - cq, -15, 15) + 15
        dr, dc, valid = np.broadcast_arrays(dr, dc, valid)
        gathered = rpb_j[:, dr, dc]
        tab = np.where(valid[None], gathered, np.float32(NEG)).astype(np.float32)
        tab = tab.reshape(16, 4, ncz, 128, 128).transpose(0, 3, 2, 1, 4)
        outs.append(np.ascontiguousarray(tab).reshape(16, 128, 4 * ncz * 128))
    return outs


def _fm(v):
    return np.ascontiguousarray(v.reshape(NCH, 128).T)


def _wtiles(w, ncol_chunks):
    N = w.shape[1]
    t = w.reshape(NCH, 128, N // 128, 128).transpose(2, 1, 0, 3)
    return np.ascontiguousarray(t).reshape(N // 128, 128, NCH * 128)


def prepare_inputs(inp):
    f32 = np.float32
    g = {k: np.asarray(v, dtype=f32) for k, v in inp.items()}
    shared = {}
    shared["wqkv"] = np.stack([_wtiles(g["w_qkv"][j], 48).reshape(3, 16, 128, 2048).transpose(1, 0, 2, 3)
                               for j in range(2)])
    shared["wo"] = np.ascontiguousarray(
        g["w_o"].reshape(2, 8, 2, 128, D).transpose(0, 1, 3, 2, 4))
    shared["wpool"] = np.ascontiguousarray(
        g["w_pool"].reshape(2, 4, 4, 128, 512).transpose(0, 3, 1, 2, 4)).reshape(2, 128, 16 * 512)
    shared["wgate"] = np.stack([_wtiles(g["w_gate"][i], NF) for i in range(DEPTH)])
    shared["wup"] = np.stack([_wtiles(g["w_up"][i], NF) for i in range(DEPTH)])
    shared["wdown"] = np.ascontiguousarray(
        g["w_down"].reshape(DEPTH, 11, 4, 128, NCH, 128).transpose(0, 1, 4, 3, 2, 5)).reshape(DEPTH, 11, 16, 128, 512)
    shared["wpg"] = np.stack([_wtiles(g["w_ple_gate"][i], 16) for i in range(DEPTH)])
    shared["wpp"] = np.ascontiguousarray(
        g["w_ple_proj"].reshape(DEPTH, 2, 128, D).transpose(0, 2, 1, 3)).reshape(DEPTH, 128, 2 * D)
    shared["ident"] = np.eye(128, dtype=f32)
    shared["bv"] = np.ascontiguousarray(np.broadcast_to(g["b_qkv"][:, None, 2 * D:3 * D], (2, 128, D)))
    tabs = {h: [_bias_tables(g["rpb"][j], h) for j in range(2)] for h in (0, 1)}
    vec_common = np.zeros((128, NVEC), f32)
    for i in range(DEPTH):
        j = i // 2
        mg = g["attn_norm_g"][j] if i % 2 == 0 else g["pool_norm_g"][j]
        vec_common[:, VC[("mix_g", i)]:VC[("mix_g", i)] + 16] = _fm(mg)
        vec_common[:, VC[("ffn_g", i)]:VC[("ffn_g", i)] + 16] = _fm(g["ffn_norm_g"][i])
        vec_common[:, VC[("ple_g", i)]:VC[("ple_g", i)] + 16] = _fm(g["ple_norm_g"][i])
        vec_common[:, VC[("ple_b", i)]:VC[("ple_b", i)] + 16] = _fm(g["b_ple_gate"][i])
    for j in range(2):
        vec_common[:, VC[("bq", j)]:VC[("bq", j)] + 16] = _fm(g["b_qkv"][j, 0:D])
        vec_common[:, VC[("bk", j)]:VC[("bk", j)] + 16] = _fm(g["b_qkv"][j, D:2 * D])
        vec_common[:, VC[("pscale", j)]:VC[("pscale", j)] + 16] = _fm(g["pool_scale"][j])
    vec_common[:, VC["final_g"]:VC["final_g"] + 16] = _fm(g["final_norm_g"])
    in_maps = []
    for core in range(8):
        b, half = core // 2, core % 2
        m = dict(shared)
        idx = _local_to_global(half, TX0)
        m["xT"] = np.ascontiguousarray(g["x"][b][idx].T).reshape(NCH, 128, TX0)
        idp = _local_to_global(half, TX)
        m["pT"] = np.ascontiguousarray(g["p"][:, b][:, idp].transpose(0, 2, 1)).reshape(DEPTH, 2, 128, TX)
        v = vec_common.copy()
        v[:, VC["cA"]] = 1.0 if half == 0 else 0.0
        v[:, VC["cB"]] = 0.0 if half == 0 else 1.0
        for gi, w in enumerate(POOL_W):
            t = np.arange(8)
            cnt = np.minimum(w, t + w // 2) if half == 0 else np.minimum(w, t + 1 + w // 2)
            v[:, VC["corr"] + gi * 8: VC["corr"] + gi * 8 + 8] = (w / cnt).astype(f32)[None, :]
        v[:, VC["eps"]] = EPS
        for hh in range(4):
            v[32 * hh:32 * hh + 32, VC["hmask"] + hh] = 1.0
        m["vecs"] = v
        for j in range(2):
            pass
        m["tab0"] = np.stack([tabs[half][j][0] for j in range(2)])
        m["tab1"] = np.stack([tabs[half][j][1] for j in range(2)])
        m["tab2"] = np.stack([tabs[half][j][2] for j in range(2)])
        in_maps.append(m)
    return in_maps


def assemble(results, T):
    out = np.zeros((4, SEQ, D), np.float32)
    for core in range(8):
        b, half = core // 2, core % 2
        o = results[core]["outT"].reshape(D, -1)[:, :T].T
        idx = _local_to_global(half, T)
        out[b, idx] = o
    return out


def kernel(**inputs):
    in_maps = prepare_inputs(inputs)
    nc, stack = build_program(None)
    with stack:
        res = run_bass_kernel_spmd(nc, in_maps, core_ids=list(range(8)))
    return assemble(res.results, 1024)
```
